# Optimizing a Trainium2 kernel written in Bass

```python
import math
import jax, jax.numpy as jnp
from jax import lax
import numpy as np

D_MODEL = 1024
BATCH = 1
SEQ = 16384
DEPTH = 4
DEC_BATCH = 8
DEC_SEQ = 4096
PAST_LEN = 128

GRID_W = 64
HEAD_DIM = 64
ATTN_HEADS = 8
ATTN_KV_HEADS = 2
RET_HEADS = 4
RET_KEY_DIM = HEAD_DIM
RET_VALUE_DIM = 2 * RET_KEY_DIM
RET_CHUNK = 128
Q_BLOCK = 128
D_FF = 2816
ROPE_THETA = 10000.0
ROPE_AXIS_PAIRS = HEAD_DIM // 4
NORM_EPS = 1e-6
ATTN_Q_W = ATTN_HEADS * HEAD_DIM
ATTN_KV_W = ATTN_KV_HEADS * HEAD_DIM
RET_QK_W = RET_HEADS * RET_KEY_DIM
RET_V_W = RET_HEADS * RET_VALUE_DIM
IN_PROJ_W = ATTN_Q_W + 2 * ATTN_KV_W + 2 * RET_QK_W + 2 * RET_V_W

kernel_name = 'hybrid_gqa_retention_macaron_encoder'


def rmsnorm(x, g):
    x32 = x.astype(jnp.float32)
    y = x32 * lax.rsqrt(jnp.mean(x32 * x32, axis=-1, keepdims=True) + NORM_EPS)
    return (y * g.astype(jnp.float32)).astype(x.dtype)


def swiglu_ffn(x, g, w13, w2):
    h = rmsnorm(x, g) @ w13
    a, b = jnp.split(h, 2, axis=-1)
    return (jax.nn.silu(a) * b) @ w2


def axial_rope_tables(T):
    rows = T // GRID_W
    row = jnp.repeat(jnp.arange(rows, dtype=jnp.float32), GRID_W)
    col = jnp.tile(jnp.arange(GRID_W, dtype=jnp.float32), rows)
    freqs = ROPE_THETA ** (-jnp.arange(ROPE_AXIS_PAIRS, dtype=jnp.float32) / ROPE_AXIS_PAIRS)
    ang = jnp.concatenate([row[:, None] * freqs, col[:, None] * freqs], axis=-1)
    return jnp.cos(ang), jnp.sin(ang)


def apply_rope(x, cos, sin):
    x32 = x.astype(jnp.float32)
    x1, x2 = jnp.split(x32, 2, axis=-1)
    c = cos[None, :, None, :]
    s = sin[None, :, None, :]
    return jnp.concatenate([x1 * c - x2 * s, x1 * s + x2 * c], axis=-1).astype(x.dtype)


def gqa_block_attention(q, k, v):
    B, T, H, hd = q.shape
    G = H // ATTN_KV_HEADS
    nb = T // Q_BLOCK
    qb = q.reshape(B, nb, Q_BLOCK, ATTN_KV_HEADS, G, hd).transpose(1, 0, 3, 4, 2, 5)
    scale = hd ** -0.5

    def block(qi):
        s = jnp.einsum('bkgqd,btkd->bkgqt', qi, k).astype(jnp.float32) * scale
        p = jax.nn.softmax(s, axis=-1).astype(v.dtype)
        return jnp.einsum('bkgqt,btkd->bkgqd', p, v)

    o = lax.map(block, qb)
    return o.transpose(1, 0, 4, 2, 3, 5).reshape(B, T, H * hd)


def retention_one_direction(q, k, v, decay_logit, include_diag):
    B, T, H, dk = q.shape
    dv = v.shape[-1]
    C = RET_CHUNK
    N = T // C
    log_gamma = jnp.log1p(-jnp.exp(decay_logit.astype(jnp.float32)))
    idx = jnp.arange(C, dtype=jnp.float32)
    diff = idx[:, None] - idx[None, :]
    mask = diff >= 0 if include_diag else diff > 0
    dmat = jnp.where(mask[None], jnp.exp(jnp.where(mask, diff, 0.0)[None] * log_gamma[:, None, None]), 0.0)
    q_decay = jnp.exp((idx[None, :] + 1.0) * log_gamma[:, None])[..., None]
    k_decay = jnp.exp((C - 1.0 - idx[None, :]) * log_gamma[:, None])[..., None]
    chunk_decay = jnp.exp(C * log_gamma)[:, None, None]

    def to_chunks(a):
        return a.reshape(B, N, C, H, a.shape[-1]).transpose(1, 0, 3, 2, 4)

    def step(S, inp):
        qi, ki, vi = inp
        inner = jnp.einsum('bhqd,bhkd->bhqk', qi, ki) * dmat
        o = jnp.einsum('bhqk,bhkv->bhqv', inner, vi) + jnp.einsum('bhqd,bhdv->bhqv', qi * q_decay, S)
        S = S * chunk_decay + jnp.einsum('bhkd,bhkv->bhdv', ki * k_decay, vi)
        return S, o

    S0 = jnp.zeros((B, H, dk, dv), jnp.float32)
    _, o = lax.scan(step, S0, (to_chunks(q), to_chunks(k), to_chunks(v)))
    return o.transpose(1, 0, 3, 2, 4).reshape(B, T, H, dv)


def bidirectional_retention(q, k, v, decay_fwd, decay_bwd):
    q32, k32, v32 = q.astype(jnp.float32), k.astype(jnp.float32), v.astype(jnp.float32)
    fwd = retention_one_direction(q32, k32, v32, decay_fwd, True)
    bwd = jnp.flip(retention_one_direction(jnp.flip(q32, 1), jnp.flip(k32, 1), jnp.flip(v32, 1), decay_bwd, False), 1)
    return fwd + bwd


def head_group_norm(y, g):
    mu = jnp.mean(y, axis=-1, keepdims=True)
    var = jnp.mean(jnp.square(y - mu), axis=-1, keepdims=True)
    yn = (y - mu) * lax.rsqrt(var + NORM_EPS)
    B, T, H, dv = y.shape
    return yn.reshape(B, T, H * dv) * g.astype(jnp.float32)


def token_mixers(h, cos, sin, w_in, q_norm, k_norm, dec_f, dec_b, ret_norm,
                 w_branch_attn, w_branch_ret, w_gate, b_gate, w_out):
    B, T, _ = h.shape
    proj = h @ w_in
    offs = [ATTN_Q_W, ATTN_Q_W + ATTN_KV_W, ATTN_Q_W + 2 * ATTN_KV_W,
            ATTN_Q_W + 2 * ATTN_KV_W + RET_QK_W, ATTN_Q_W + 2 * ATTN_KV_W + 2 * RET_QK_W,
            ATTN_Q_W + 2 * ATTN_KV_W + 2 * RET_QK_W + RET_V_W]
    aq, ak, av, rq, rk, rv, rg = jnp.split(proj, offs, axis=-1)
    aq = apply_rope(rmsnorm(aq.reshape(B, T, ATTN_HEADS, HEAD_DIM), q_norm), cos, sin)
    ak = apply_rope(rmsnorm(ak.reshape(B, T, ATTN_KV_HEADS, HEAD_DIM), k_norm), cos, sin)
    av = av.reshape(B, T, ATTN_KV_HEADS, HEAD_DIM)
    ya = gqa_block_attention(aq, ak, av) @ w_branch_attn
    rq = apply_rope(rq.reshape(B, T, RET_HEADS, RET_KEY_DIM), cos, sin)
    rk = apply_rope(rk.reshape(B, T, RET_HEADS, RET_KEY_DIM), cos, sin) * (RET_KEY_DIM ** -0.5)
    rv = rv.reshape(B, T, RET_HEADS, RET_VALUE_DIM)
    yr = head_group_norm(bidirectional_retention(rq, rk, rv, dec_f, dec_b), ret_norm).astype(h.dtype)
    yr = (jax.nn.silu(rg) * yr) @ w_branch_ret
    gates = jax.nn.sigmoid((h @ w_gate + b_gate).astype(jnp.float32)).astype(h.dtype)
    g_a, g_r = jnp.split(gates, 2, axis=-1)
    return (g_a * ya + g_r * yr) @ w_out


def trunk(x, ffn1_norm, ffn1_w13, ffn1_w2, mix_norm, w_in, q_norm, k_norm,
          ret_decay_fwd, ret_decay_bwd, ret_norm, w_branch_attn, w_branch_ret,
          w_gate, b_gate, w_out, ffn2_norm, ffn2_w13, ffn2_w2, final_norm):
    T = x.shape[1]
    cos, sin = axial_rope_tables(T)
    for l in range(DEPTH):
        x = x + 0.5 * swiglu_ffn(x, ffn1_norm[l], ffn1_w13[l], ffn1_w2[l])
        h = rmsnorm(x, mix_norm[l])
        x = x + token_mixers(h, cos, sin, w_in[l], q_norm[l], k_norm[l], ret_decay_fwd[l],
                             ret_decay_bwd[l], ret_norm[l], w_branch_attn[l], w_branch_ret[l],
                             w_gate[l], b_gate[l], w_out[l])
        x = x + 0.5 * swiglu_ffn(x, ffn2_norm[l], ffn2_w13[l], ffn2_w2[l])
    return rmsnorm(x, final_norm)


def setup_inputs(seed: int = 0) -> dict:
    key = jax.random.key(seed)
    ks = jax.random.split(key, 24)

    def nrm(k, shape, scale):
        return jax.random.normal(k, shape, jnp.float32) * scale

    base_decay = -(5.0 + jnp.arange(RET_HEADS, dtype=jnp.float32)) * math.log(2.0)
    return {
        'x_prompt': nrm(ks[0], (BATCH, SEQ, D_MODEL), 1.0),
        'x_sample': nrm(ks[1], (DEC_BATCH, DEC_SEQ, D_MODEL), 1.0),
        'ffn1_norm': 1.0 + nrm(ks[2], (DEPTH, D_MODEL), 0.02),
        'ffn1_w13': nrm(ks[3], (DEPTH, D_MODEL, 2 * D_FF), D_MODEL ** -0.5),
        'ffn1_w2': nrm(ks[4], (DEPTH, D_FF, D_MODEL), D_FF ** -0.5),
        'mix_norm': 1.0 + nrm(ks[5], (DEPTH, D_MODEL), 0.02),
        'w_in': nrm(ks[6], (DEPTH, D_MODEL, IN_PROJ_W), D_MODEL ** -0.5),
        'q_norm': 1.0 + nrm(ks[7], (DEPTH, HEAD_DIM), 0.02),
        'k_norm': 1.0 + nrm(ks[8], (DEPTH, HEAD_DIM), 0.02),
        'ret_decay_fwd': base_decay + nrm(ks[9], (DEPTH, RET_HEADS), 0.05),
        'ret_decay_bwd': base_decay + nrm(ks[10], (DEPTH, RET_HEADS), 0.05),
        'ret_norm': 1.0 + nrm(ks[11], (DEPTH, RET_V_W), 0.02),
        'w_branch_attn': nrm(ks[12], (DEPTH, ATTN_Q_W, D_MODEL), ATTN_Q_W ** -0.5),
        'w_branch_ret': nrm(ks[13], (DEPTH, RET_V_W, D_MODEL), RET_V_W ** -0.5),
        'w_gate': nrm(ks[14], (DEPTH, D_MODEL, 2 * D_MODEL), D_MODEL ** -0.5),
        'b_gate': nrm(ks[15], (DEPTH, 2 * D_MODEL), 0.02),
        'w_out': nrm(ks[16], (DEPTH, D_MODEL, D_MODEL), D_MODEL ** -0.5),
        'ffn2_norm': 1.0 + nrm(ks[17], (DEPTH, D_MODEL), 0.02),
        'ffn2_w13': nrm(ks[18], (DEPTH, D_MODEL, 2 * D_FF), D_MODEL ** -0.5),
        'ffn2_w2': nrm(ks[19], (DEPTH, D_FF, D_MODEL), D_FF ** -0.5),
        'final_norm': 1.0 + nrm(ks[20], (D_MODEL,), 0.02),
    }


def reference(x_prompt, x_sample, ffn1_norm, ffn1_w13, ffn1_w2, mix_norm, w_in, q_norm, k_norm,
              ret_decay_fwd, ret_decay_bwd, ret_norm, w_branch_attn, w_branch_ret,
              w_gate, b_gate, w_out, ffn2_norm, ffn2_w13, ffn2_w2, final_norm):
    y_prompt = trunk(x_prompt, ffn1_norm, ffn1_w13, ffn1_w2, mix_norm, w_in, q_norm, k_norm,
                     ret_decay_fwd, ret_decay_bwd, ret_norm, w_branch_attn, w_branch_ret,
                     w_gate, b_gate, w_out, ffn2_norm, ffn2_w13, ffn2_w2, final_norm)
    y_sample = trunk(x_sample, ffn1_norm, ffn1_w13, ffn1_w2, mix_norm, w_in, q_norm, k_norm,
                     ret_decay_fwd, ret_decay_bwd, ret_norm, w_branch_attn, w_branch_ret,
                     w_gate, b_gate, w_out, ffn2_norm, ffn2_w13, ffn2_w2, final_norm)
    return (y_prompt, y_sample)
```

```python
import math
import os
from contextlib import ExitStack

import numpy as np
import ml_dtypes

import concourse.bass as bass
import concourse.mybir as mybir
from concourse.bass_utils import run_bass_kernel_spmd

F32 = mybir.dt.float32
BF16 = mybir.dt.bfloat16
AF = mybir.ActivationFunctionType
ALU = mybir.AluOpType
AX = mybir.AxisListType

NCORES = 8
D = 1024
DFF = 2816
NJ = DFF // 128
HD = 64
C = 128
TT = 512
EPS = 1e-6
GRID_W = 64
ROPE_THETA = 10000.0

WSIZES = [("f1w13", 45056), ("f1w2", 22528), ("winfm", 26624), ("wrv", 4096), ("wav", 1024),
          ("wrg", 4096), ("wgate", 16384), ("wba", 4096), ("wbr", 4096), ("wo", 8192),
          ("f2w13", 45056), ("f2w2", 22528)]
WOFF = {}
_o = 0
for _n, _s in WSIZES:
    WOFF[_n] = _o
    _o += _s
FL = _o
A_NAMES = ["f1w13", "f1w2", "winfm", "wrv", "wav"]
FA = sum(sz for n, sz in WSIZES if n in A_NAMES)
FB = FL - FA
WPART = {n: (1 if n in A_NAMES else 0) for n, _ in WSIZES}
WREL = {n: (WOFF[n] if n in A_NAMES else WOFF[n] - FA) for n, _ in WSIZES}
SLOT = 4096
NSLOT = 5


class Sem:
    def __init__(self, h):
        self.h = h
        self.n = 0


class Buf:
    __slots__ = ("w", "r", "ro", "excl")

    def __init__(self, ro=False, excl=False):
        self.w = {}
        self.r = {}
        self.ro = ro
        self.excl = excl


class Eng:
    def __init__(self, name):
        self.name = name
        self.ops = []
        self.waited = {}
        self.sem = None
        self.pend_r = []
        self.pend_w = []


class Prog:
    def __init__(self, nc, stack):
        self.nc = nc
        self.stack = stack
        self.nsem = 0
        self.E = {n: Eng(n) for n in ["pe", "act", "dve", "pool", "sp"]}
        for n in ["pe", "act", "dve", "pool"]:
            self.E[n].sem = self.newsem()
        self.dsems = {"sp": [self.newsem() for _ in range(28)], "pool": [self.newsem() for _ in range(10)]}
        self.di = {"sp": 0, "pool": 0}
        self.agsems = []

    def newsem(self):
        h = self.stack.enter_context(self.nc.semaphore(f"s{self.nsem}"))
        self.nsem += 1
        return Sem(h)

    def _waits(self, e, reads, writes):
        need = {}
        for b in reads:
            for s, v in b.w.items():
                if need.get(s, 0) < v:
                    need[s] = v
            if b.excl:
                for s, v in b.r.items():
                    if s is not e.sem and need.get(s, 0) < v:
                        need[s] = v
        for b in writes:
            for dct in (b.w, b.r):
                for s, v in dct.items():
                    if need.get(s, 0) < v:
                        need[s] = v
        for s, v in need.items():
            if e.name == "pe" and s is e.sem:
                continue
            if e.waited.get(s, 0) < v:
                e.ops.append(("w", s.h, v))
                e.waited[s] = v

    def op(self, en, fn, reads=(), writes=(), inc=True):
        e = self.E[en]
        self._waits(e, reads, writes)
        if inc:
            if e.sem.n >= 12000:
                e.sem = self.newsem()
            e.sem.n += 1
            s, v = e.sem, e.sem.n
            e.ops.append(("i", fn, s.h))
            for b in list(reads) + e.pend_r:
                if not b.ro:
                    b.r[s] = v
            for b in list(writes) + e.pend_w:
                b.w = {s: v}
                b.r = {}
            e.pend_r = []
            e.pend_w = []
        else:
            e.ops.append(("i", fn, None))
            e.pend_r += [b for b in reads if not b.ro]
            e.pend_w += list(writes)

    def dma(self, q, out, in_, reads=(), writes=()):
        e = self.E[q]
        sems = self.dsems[q]
        s = sems[self.di[q] % len(sems)]
        self.di[q] += 1
        if s.n > 0 and e.waited.get(s, 0) < s.n:
            e.ops.append(("w", s.h, s.n))
            e.waited[s] = s.n
        self._waits(e, reads, writes)
        s.n += 16
        e.ops.append(("d", out, in_, s.h))
        for b in reads:
            if not b.ro:
                b.r[s] = s.n
        for b in writes:
            b.w = {s: s.n}
            b.r = {}

    def allgather(self, in_t, out_t, reads=(), writes=()):
        e = self.E["pool"]
        self._waits(e, reads, writes)
        s = self.newsem()
        s.n = 1
        self.agsems.append(s)
        e.ops.append(("c", in_t, out_t, s.h))
        for b in reads:
            if not b.ro:
                b.r[s] = 1
        for b in writes:
            b.w = {s: 1}
            b.r = {}

    def final_wait(self, en, bufs):
        e = self.E[en]
        self._waits(e, bufs, bufs)
        for q in ("sp", "pool"):
            for s in self.dsems[q]:
                if s.n > 0 and e.waited.get(s, 0) < s.n:
                    e.ops.append(("w", s.h, s.n))
                    e.waited[s] = s.n
        for s in self.agsems:
            e.ops.append(("w", s.h, 1))
        for n in ("pe", "act", "dve", "pool"):
            s = self.E[n].sem
            if s.n > 0:
                e.ops.append(("w", s.h, s.n))

    def replay(self, eng, en):
        for o in self.E[en].ops:
            k = o[0]
            if k == "w":
                eng.wait_ge(o[1], o[2])
            elif k == "i":
                ins = o[1](eng)
                if o[2] is not None:
                    ins.then_inc(o[2], 1)
            elif k == "d":
                eng.dma_start(out=o[1], in_=o[2]).then_inc(o[3], 16)
            elif k == "c":
                eng.collective_compute(
                    "AllGather", ALU.bypass, replica_groups=[list(range(NCORES))],
                    ins=[o[1].ap().opt()], outs=[o[2].ap().opt()],
                ).then_inc(o[3])


def _lhsT_chunks(W, cols_list):
    K = W.shape[0]
    kc = K // 128
    Wr = W.reshape(kc, 128, -1)
    out = np.empty((128, len(cols_list), kc, 128), np.float32)
    for i, cols in enumerate(cols_list):
        out[:, i] = Wr[:, :, cols].transpose(1, 0, 2)
    return out


def _perm64(cols):
    cols = np.asarray(cols)
    return np.concatenate([cols[32:64], cols[0:32]])


def arrange_layer(inp, l):
    ar = np.arange
    parts = {}
    for k, n13, n2 in (("f1", "ffn1_w13", "ffn1_w2"), ("f2", "ffn2_w13", "ffn2_w2")):
        W13 = inp[n13][l]
        cl = []
        for j in range(NJ):
            cl.append(ar(128 * j, 128 * j + 128))
            cl.append(DFF + ar(128 * j, 128 * j + 128))
        parts[k + "w13"] = _lhsT_chunks(W13, cl).reshape(128, -1)
        parts[k + "w2"] = _lhsT_chunks(inp[n2][l], [ar(128 * n, 128 * n + 128) for n in range(8)]).reshape(128, -1)
    Win = inp["w_in"][l]
    cl = []
    for c in range(4):
        h0 = ar(64 * c, 64 * c + 64)
        h1 = ar(64 * (4 + c), 64 * (4 + c) + 64)
        cl.append(np.concatenate([h0, h1]))
        cl.append(np.concatenate([_perm64(h0), _perm64(h1)]))
    k0 = 512 + ar(0, 64)
    k1 = 512 + ar(64, 128)
    cl.append(np.concatenate([k0, k1]))
    cl.append(np.concatenate([_perm64(k0), _perm64(k1)]))
    for base in (768, 1024):
        for h in range(4):
            hc = base + ar(64 * h, 64 * h + 64)
            cl.append(np.concatenate([hc, hc]))
            cl.append(np.concatenate([_perm64(hc), _perm64(hc)]))
    parts["winfm"] = _lhsT_chunks(Win, cl).reshape(128, -1)
    Wr = Win.reshape(8, 128, -1)
    parts["wrv"] = Wr[:, :, 1280:1792].transpose(1, 0, 2).reshape(128, -1)
    parts["wav"] = Wr[:, :, 640:768].transpose(1, 0, 2).reshape(128, -1)
    parts["wrg"] = Wr[:, :, 1792:2304].transpose(1, 0, 2).reshape(128, -1)
    cl = []
    for n in range(8):
        cl.append(ar(128 * n, 128 * n + 128))
        cl.append(1024 + ar(128 * n, 128 * n + 128))
    parts["wgate"] = _lhsT_chunks(inp["w_gate"][l], cl).reshape(128, -1)
    rows = []
    for c in range(4):
        rows.append(ar(64 * c, 64 * c + 64))
        rows.append(ar(64 * (4 + c), 64 * (4 + c) + 64))
    rows = np.concatenate(rows)
    parts["wba"] = inp["w_branch_attn"][l][rows].reshape(4, 128, 1024).transpose(1, 0, 2).reshape(128, -1)
    parts["wbr"] = inp["w_branch_ret"][l].reshape(4, 128, 1024).transpose(1, 0, 2).reshape(128, -1)
    parts["wo"] = inp["w_out"][l].reshape(8, 128, 1024).transpose(1, 0, 2).reshape(128, -1)
    blob = np.empty((128, FL), np.float32)
    for n, s in WSIZES:
        assert parts[n].shape == (128, s), (n, parts[n].shape)
        blob[:, WOFF[n]:WOFF[n] + s] = parts[n]
    return blob


def rope_tables(pos):
    pos = np.asarray(pos)
    row = (pos // GRID_W).astype(np.float32)
    col = (pos % GRID_W).astype(np.float32)
    freqs = (ROPE_THETA ** (-np.arange(16, dtype=np.float32) / 16)).astype(np.float32)
    ang = np.concatenate([row[:, None] * freqs, col[:, None] * freqs], axis=-1)
    cos = np.cos(ang).astype(np.float32).T
    sin = np.sin(ang).astype(np.float32).T
    c64 = np.concatenate([cos, cos], 0)
    s64 = np.concatenate([-sin, sin], 0)
    out = np.empty((128, 2, len(pos)), np.float32)
    out[:, 0] = np.concatenate([c64, c64], 0)
    out[:, 1] = np.concatenate([s64, s64], 0)
    return out


def const_tables(core, nch_p):
    i = np.arange(128, dtype=np.float32)
    jj = i[:, None]
    ii = i[None, :]
    t = np.zeros((128, 9, 128), np.float32)
    t[:, 0] = np.maximum(ii - jj, 0)
    t[:, 1] = (ii >= jj)
    t[:, 2] = np.maximum(jj - ii, 0)
    t[:, 3] = (jj > ii)
    t[0:64, 4] = (ii + 1.0)
    t[64:128, 4] = (C - ii)
    t[:, 5, 0:64] = (C - 1.0 - jj)
    t[:, 5, 64:128] = jj
    t[:, 6] = float(C)
    cp = np.arange(8, dtype=np.float32)
    ef = nch_p * C * (core - 1 - cp)
    mf = (cp < core)
    eb = nch_p * C * (cp - core - 1)
    mb = (cp > core)
    t[0:64, 7, 0:8] = np.where(mf, ef, 0.0)
    t[64:128, 7, 0:8] = np.where(mb, eb, 0.0)
    t[0:64, 8, 0:8] = mf
    t[64:128, 8, 0:8] = mb
    return t


def build_program(NT_S, NT_P, KIND):
    DEPTH = 2
    RUN_B = KIND in ("mid", "last")
    RUN_A = KIND in ("first", "mid")
    NT = NT_S + NT_P
    T_S = NT_S * TT
    T_PL = NT_P * TT
    T_P = T_PL * NCORES
    NCH = NT * 4
    NCH_P = NT_P * 4
    nc = bass.Bass("TRN2", target_bir_lowering=False)
    STAGE = 9
    PL = "dve"
    KSUB = 9
    stack = ExitStack()
    P = Prog(nc, stack)

    def dram(name, shape, dt, kind=None):
        if kind:
            return nc.dram_tensor(name, shape, dt, kind=kind)
        return nc.dram_tensor(name, shape, dt)

    KIN = "ExternalInput" if RUN_B else None
    KOUT = "ExternalOutput" if RUN_A else None

    def pair(name, shape, dt):
        return [dram(name + "0", shape, dt, KIN), dram(name + "1", shape, dt, KOUT)]

    x_in = dram("x_in", [NT * TT, D], F32, "ExternalInput") if KIND == "first" else None
    y_out = dram("y_out", [NT * TT, D], F32, "ExternalOutput") if KIND == "last" else None
    wf = [dram("wf0", [128, FB], F32, "ExternalInput") if RUN_B else None,
          dram("wf1", [128, FA], F32, "ExternalInput") if RUN_A else None]
    cs_in = dram("cs", [NT, 128, 2, TT], F32, "ExternalInput")
    ct_in = dram("ctab", [128, 9, 128], F32, "ExternalInput")
    nrm_in = dram("norms", [128, DEPTH * 3 + 1, 8], F32, "ExternalInput")
    qk_in = dram("qkg", [128, DEPTH, 4], F32, "ExternalInput")
    bg_in = dram("bgate", [128, DEPTH, 16], F32, "ExternalInput")
    rg_in = dram("retg", [128, DEPTH, 512], F32, "ExternalInput")
    dec_in = dram("dec", [128, DEPTH, 2, 4], F32, "ExternalInput")
    id_in = dram("ident", [128, 128], F32, "ExternalInput")
    wb = [dram("wb0", [128, FB], BF16), dram("wb1", [128, FA], BF16)]
    Xd = dram("Xd", [NT, 128, 8, TT], F32)
    Xe = pair("Xe", [NT, 128, 8, TT], F32)
    Hd = pair("Hd", [NT, 128, 8, TT], BF16)
    Qd = pair("Qd", [NT, 128, 4, TT], BF16)
    RQd = pair("RQd", [NT, 128, 4, TT], BF16)
    RKd = pair("RKd", [NT, 128, 4, TT], BF16)
    RVd = pair("RVd", [NT, 128, 4, 512], BF16)
    KTs = pair("KTs", [128, T_S], BF16)
    VEs = pair("VEs", [T_S, 256], BF16)
    KTl = [None, dram("KTl1", [128, T_PL], BF16, KOUT)]
    VEl = [None, dram("VEl1", [T_PL, 256], BF16, KOUT)]
    KTg = [dram("KTg0", [NCORES * 128, T_PL], BF16, KIN), None]
    VEg = [dram("VEg0", [T_P, 256], BF16, KIN), None]
    KVd = pair("KVd", [NCH, 128, 512], F32)
    STd = dram("STd", [NCH, 128, 512], BF16)
    AGi = [None, dram("AGi1", [128, 512], F32, KOUT)]
    AGo = [dram("AGo0", [NCORES * 128, 512], F32, KIN), None]

    bXd = [Buf() for _ in range(NT)]
    bXe = [[Buf() for _ in range(NT)] for _ in range(2)]
    bH = [[Buf() for _ in range(NT)] for _ in range(2)]
    bQ = [[Buf() for _ in range(NT)] for _ in range(2)]
    bRQ = [[Buf() for _ in range(NT)] for _ in range(2)]
    bRK = [[Buf() for _ in range(NT)] for _ in range(2)]
    bRV = [[Buf() for _ in range(NT)] for _ in range(2)]
    bKTs = [Buf() for _ in range(2)]
    bVEs = [Buf() for _ in range(2)]
    bKTl = [Buf() for _ in range(2)]
    bVEl = [Buf() for _ in range(2)]
    bKTg = [Buf() for _ in range(2)]
    bVEg = [Buf() for _ in range(2)]
    bKV = [[Buf() for _ in range(NCH)] for _ in range(2)]
    bST = [Buf() for _ in range(NCH)]
    bAGi = [Buf() for _ in range(2)]
    bAGo = [Buf() for _ in range(2)]
    bWb = [Buf(ro=True) for _ in range(DEPTH)]
    bYs = []

    def sb(name, shape, dt):
        return stack.enter_context(nc.sbuf_tensor(name, shape, dt))

    ws = [sb(f"ws{i}", [128, SLOT], BF16) for i in range(NSLOT)]
    bws = [Buf() for _ in range(NSLOT)]
    xT = sb("xT", [128, 8, TT], F32)
    bx = [Buf() for _ in range(8)]
    hT = sb("hT", [128, 8, TT], BF16)
    bh = [Buf() for _ in range(8)]
    xn = sb("xn", [128, 8, TT], BF16)
    bxn = [Buf() for _ in range(8)]
    gT = sb("gT", [128, NJ, TT], BF16)
    bg = [Buf() for _ in range(NJ)]
    QT = sb("QT", [128, 4, TT], BF16)
    bQT = [Buf() for _ in range(4)]
    RQ = sb("RQ", [128, 4, TT], BF16)
    bRQs = [Buf() for _ in range(4)]
    RK = sb("RK", [128, 4, TT], BF16)
    bRKs = [Buf() for _ in range(4)]
    RV = sb("RV", [128, 4, 512], BF16)
    bRVs = [Buf() for _ in range(4)]
    ST = sb("ST", [128, 4, 512], BF16)
    bSTs = [Buf() for _ in range(4)]
    CS = sb("CS", [128, 2, TT], F32)
    bCS = Buf()
    KO = sb("KO", [128, TT], BF16)
    bKO = Buf()
    VO = sb("VO", [128, 4, 256], BF16)
    bVO = Buf()
    NKV = 3
    KS = [sb(f"KS{i}", [128, TT], BF16) for i in range(NKV)]
    VS = [sb(f"VS{i}", [128, 4, 256], BF16) for i in range(NKV)]
    bKVs = [Buf() for _ in range(NKV)]
    bKVsV = [Buf() for _ in range(NKV)]
    NPT = 4
    PT = [sb(f"PT{i}", [128, TT], BF16) for i in range(NPT)]
    bPT = [Buf() for _ in range(NPT)]
    attnT = sb("attnT", [128, 4, TT], BF16)
    battn = [Buf() for _ in range(4)]
    yrT = sb("yrT", [128, 4, TT], BF16)
    byr = [Buf() for _ in range(4)]
    NTMP = 6
    TM = [sb(f"TM{i}", [128, TT], F32) for i in range(NTMP)]
    bTM = [Buf() for _ in range(NTMP)]
    tmi = [0]

    def tmp():
        i = tmi[0] % NTMP
        tmi[0] += 1
        return TM[i], bTM[i]

    XI = [sb(f"XI{i}", [128, D], F32) for i in range(2)]
    bXI = [Buf() for _ in range(2)]
    DT_ = sb("DTab", [128, 4, 128], F32)
    QD = sb("QDtab", [128, 4, 128], F32)
    KD = sb("KDtab", [128, 4, 128], F32)
    GD = sb("GDtab", [128, 4, 128], F32)
    CO = sb("COtab", [128, 4, 8], F32)
    bDec = Buf()
    RGt = sb("RGt", [128, 512], F32)
    bRGt = Buf()
    ctab = sb("ctab_s", [128, 9, 128], F32)
    nrm = sb("nrm_s", [128, DEPTH * 3 + 1, 8], F32)
    qkg = sb("qkg_s", [128, DEPTH, 4], F32)
    bgs = sb("bg_s", [128, DEPTH, 16], F32)
    decs = sb("dec_s", [128, DEPTH, 2, 4], F32)
    lgall = sb("lgall", [128, 2, 4], F32)
    lgcat = sb("lgcat", [128, 4], F32)
    identf = sb("identf", [128, 128], F32)
    identb = sb("identb", [128, 128], BF16)
    onesM = sb("onesM", [128, 128], BF16)
    blk1 = sb("blk1", [128, 128], BF16)
    epsb = sb("epsb", [128, 1], F32)
    oneb = sb("oneb", [128, 1], F32)
    sm = sb("smalls", [128, 64], F32)
    bsm = Buf()
    Sacc = sb("Sacc", [128, 512], F32)
    bSacc = Buf()
    Sin = sb("Sin", [128, 512], F32)
    bSin = Buf()
    bconst = Buf()

    PB = [stack.enter_context(nc.psum_tensor(f"pb{i}", [128, 512], F32)) for i in range(8)]
    bPB = [Buf(excl=True) for _ in range(8)]

    seq = []

    def seq_ffn(l, k):
        for i in range(11):
            seq.append((l, WREL[k + "w13"] + i * 4096, 4096, (k + "w13", i)))
        for n in range(8):
            seq.append((l, WREL[k + "w2"] + n * 2816, 2816, (k + "w2", n)))

    order_A = list(range(NT_S, NT)) + list(range(NT_S))
    order_B = list(range(NT))
    if RUN_B:
        for t in order_B:
            seq.append((0, WREL["wrg"], 4096, ("wrg", 0)))
            seq.append((0, WREL["wba"], 4096, ("wba", 0)))
            seq.append((0, WREL["wbr"], 4096, ("wbr", 0)))
            for n in range(8):
                seq.append((0, WREL["wgate"] + n * 2048, 2048, ("wgate", n)))
            seq.append((0, WREL["wo"], 4096, ("wo", 0)))
            seq.append((0, WREL["wo"] + 4096, 4096, ("wo", 1)))
            seq_ffn(0, "f2")
    if RUN_A:
        for t in order_A:
            seq_ffn(1, "f1")
            for i in range(13):
                seq.append((1, WREL["winfm"] + i * 2048, 2048, ("winfm", i)))
            seq.append((1, WREL["wrv"], 4096, ("wrv", 0)))
            seq.append((1, WREL["wav"], 1024, ("wav", 0)))

    class WL:
        free = list(range(NSLOT))
        nload = 0
        nuse = 0
        slot_of = {}

    def wl_topup():
        while WL.free and WL.nload < len(seq) and WL.nload < WL.nuse + NSLOT:
            l, off, size, key = seq[WL.nload]
            s = WL.free.pop(0)
            P.dma("sp", ws[s][:, 0:size], wb[l].ap()[:, off:off + size], reads=[bWb[l]], writes=[bws[s]])
            WL.slot_of[WL.nload] = s
            WL.nload += 1

    def wget(l, key):
        i = WL.nuse
        while not (seq[i][0] == l and seq[i][3] == key):
            assert STAGE < 9 or KSUB < 9, (seq[i], l, key)
            if i >= WL.nload:
                wl_topup()
            s_ = WL.slot_of.pop(i)
            WL.nuse += 1
            WL.free.append(s_)
            i = WL.nuse
        if i >= WL.nload:
            wl_topup()
        assert i < WL.nload, "weight loader deadlock"
        WL.nuse += 1
        s = WL.slot_of.pop(i)
        return s

    def wrel(s):
        WL.free.append(s)
        wl_topup()

    def mm(out, lhsT, rhs, start, stop, reads, writes, inc):
        P.op("pe", lambda e: e.matmul(out, lhsT=lhsT, rhs=rhs, start=start, stop=stop),
             reads=reads, writes=writes, inc=inc)

    def act(out, in_, func, reads, writes, bias=None, scale=None):
        kw = {}
        if bias is not None:
            kw["bias"] = bias
        if scale is not None:
            kw["scale"] = scale
        P.op("act", lambda e: e.activation(out=out, in_=in_, func=func, **kw), reads=reads, writes=writes)

    def tt(en, out, in0, in1, op, reads, writes):
        P.op(en, lambda e: e.tensor_tensor(out=out, in0=in0, in1=in1, op=op), reads=reads, writes=writes)

    def stt(en, out, in0, scalar, in1, op0, op1, reads, writes):
        P.op(en, lambda e: e.scalar_tensor_tensor(out=out, in0=in0, scalar=scalar, in1=in1, op0=op0, op1=op1),
             reads=reads, writes=writes)

    def ts(en, out, in0, s1, s2, op0, op1, reads, writes):
        P.op(en, lambda e: e.tensor_scalar(out=out, in0=in0, scalar1=s1, scalar2=s2, op0=op0, op1=op1),
             reads=reads, writes=writes)

    def cp(en, out, in_, reads, writes):
        if en == "act":
            P.op(en, lambda e: e.activation(out=out, in_=in_, func=AF.Copy), reads=reads, writes=writes)
        else:
            P.op(en, lambda e: e.tensor_copy(out=out, in_=in_), reads=reads, writes=writes)

    def tsm(en, out, in0, s1, reads, writes):
        P.op(en, lambda e: e.tensor_scalar_mul(out=out, in0=in0, scalar1=s1), reads=reads, writes=writes)

    def rsqrt_act(out, in_, reads, writes, nrows=128):
        act(out, in_, AF.Ln, reads, writes, bias=epsb[0:nrows, 0:1], scale=1.0)
        act(out, out, AF.Exp, writes, writes, scale=-0.5)

    for l in range(DEPTH):
        if wf[l] is None:
            continue
        for r in range(8):
            P.dma("pool", wb[l].ap()[16 * r:16 * r + 16, :], wf[l].ap()[16 * r:16 * r + 16, :], writes=[bWb[l]])
    P.dma("sp", ctab[:], ct_in.ap()[:, :, :], writes=[bconst])
    P.dma("sp", nrm[:], nrm_in.ap()[:, :, :], writes=[bconst])
    P.dma("sp", qkg[:], qk_in.ap()[:, :, :], writes=[bconst])
    P.dma("sp", bgs[:], bg_in.ap()[:, :, :], writes=[bconst])
    P.dma("sp", decs[:], dec_in.ap()[:, :, :, :], writes=[bconst])
    P.dma("sp", identf[:], id_in.ap()[:, :], writes=[bconst])
    P.op("dve", lambda e: e.tensor_copy(out=identb[:], in_=identf[:]), reads=[bconst], writes=[bconst])
    P.op("dve", lambda e: e.memset(onesM[:], 1.0 / D), writes=[bconst])
    P.op("dve", lambda e: e.memset(blk1[:], 0.0), writes=[bconst])
    P.op("dve", lambda e: e.memset(blk1[0:64, 0:64], 1.0 / HD), writes=[bconst])
    P.op("dve", lambda e: e.memset(blk1[64:128, 64:128], 1.0 / HD), writes=[bconst])
    P.op("dve", lambda e: e.memset(epsb[:], EPS), writes=[bconst])
    P.op("dve", lambda e: e.memset(oneb[:], 1.0), writes=[bconst])
    P.op("dve", lambda e: e.memset(VO[:], 1.0), writes=[bVO])

    xi_n = [0]
    for t in (range(NT) if KIND == "first" else []):
        for sub in range(4):
            xi = xi_n[0] % 2
            xi_n[0] += 1
            r0 = t * TT + sub * 128
            P.dma("sp", XI[xi][:], x_in.ap()[r0:r0 + 128, :], writes=[bXI[xi]])
            for g in range(2):
                pb = (sub * 2 + g) % 7
                for cc in range(4):
                    c = g * 4 + cc
                    P.op("pe", lambda e, pb=pb, cc=cc, c=c, xi=xi: e.transpose(
                        PB[pb][:, cc * 128:(cc + 1) * 128], XI[xi][:, c * 128:(c + 1) * 128], identf[:]),
                        reads=[bXI[xi], bconst], writes=[bPB[pb]], inc=(cc == 3))
                src = PB[pb][:].rearrange("p (c t) -> p c t", c=4)
                cp("dve" if g == 0 else "act", xT[:, g * 4:(g + 1) * 4, sub * 128:(sub + 1) * 128], src,
                   reads=[bPB[pb]], writes=bx[g * 4:(g + 1) * 4])
        P.dma("sp", Xd.ap()[t], xT[:], reads=bx, writes=[bXd[t]])

    def rmsnorm_fm(gidx, dst, bdst, dst_f32=False):
        pb = 6
        for c in range(8):
            sq = PT[c % NPT]
            act(sq[:], xT[:, c, :], AF.Square, [bx[c]], [bPT[c % NPT]])
            mm(PB[pb][:], onesM[:], sq[:], c == 0, c == 7, [bPT[c % NPT], bconst], [bPB[pb]], True)
        rs, brs = tmp()
        rsqrt_act(rs[:], PB[pb][:], [bPB[pb]], [brs])
        for c in range(8):
            stt("dve", dst[:, c, :], xT[:, c, :], nrm[:, gidx, c:c + 1], rs[:], ALU.mult, ALU.mult,
                [bx[c], brs, bconst], [bdst[c]])

    def ffn(l, k, gidx):
        rmsnorm_fm(gidx, xn, bxn)
        for i in range(11):
            s = wget(l, (k + "w13", i))
            w = ws[s][:].rearrange("p (j a k m) -> p j a k m", j=2, a=2, k=8)
            for jj in range(2):
                j = 2 * i + jj
                pa = (2 * (j % 3)) % 7
                pbk = pa + 1
                for a, pbx in ((0, pa), (1, pbk)):
                    for kc in range(8):
                        mm(PB[pbx][:], w[:, jj, a, kc, :], xn[:, kc, :], kc == 0, kc == 7,
                           [bws[s], bxn[kc]], [bPB[pbx]], kc == 7)
                sa, bsa = tmp()
                act(sa[:], PB[pa][:], AF.Silu, [bPB[pa]], [bsa])
                tt("dve", gT[:, j, :], sa[:], PB[pbk][:], ALU.mult, [bsa, bPB[pbk]], [bg[j]])
            wrel(s)
        for n in range(8):
            s = wget(l, (k + "w2", n))
            w = ws[s][:, 0:2816].rearrange("p (j m) -> p j m", j=NJ)
            pb = n % 6
            for j in range(NJ):
                mm(PB[pb][:], w[:, j, :], gT[:, j, :], j == 0, j == NJ - 1, [bws[s], bg[j]], [bPB[pb]], j == NJ - 1)
            wrel(s)
            stt("dve", xT[:, n, :], PB[pb][:], 0.5, xT[:, n, :], ALU.mult, ALU.add, [bPB[pb], bx[n]], [bx[n]])

    def layer_tables(l):
        rd = [bconst, bDec]
        wr = [bDec]
        act(lgall[:], decs[:, l, :, :], AF.Exp, rd, wr)
        act(lgall[:], lgall[:], AF.Ln, rd, wr, bias=oneb[:, 0:1], scale=-1.0)
        cp("dve", lgcat[0:64, :], lgall[0:64, 0, :], rd, wr)
        cp("dve", lgcat[64:128, :], lgall[64:128, 1, :], rd, wr)
        for h in range(4):
            t1, b1 = tmp()
            t2, b2 = tmp()
            act(t1[:, 0:128], ctab[:, 0, :], AF.Exp, rd, [b1], scale=lgall[:, 0, h:h + 1])
            tt("dve", t1[:, 0:128], t1[:, 0:128], ctab[:, 1, :], ALU.mult, [b1, bconst], [b1])
            act(t2[:, 0:128], ctab[:, 2, :], AF.Exp, rd, [b2], scale=lgall[:, 1, h:h + 1])
            tt("dve", t2[:, 0:128], t2[:, 0:128], ctab[:, 3, :], ALU.mult, [b2, bconst], [b2])
            tt("dve", DT_[:, h, :], t1[:, 0:128], t2[:, 0:128], ALU.add, [b1, b2] + rd, wr)
            act(QD[:, h, :], ctab[:, 4, :], AF.Exp, rd, wr, scale=lgcat[:, h:h + 1])
            act(KD[:, h, 0:64], ctab[:, 5, 0:64], AF.Exp, rd, wr, scale=lgall[:, 0, h:h + 1])
            act(KD[:, h, 64:128], ctab[:, 5, 64:128], AF.Exp, rd, wr, scale=lgall[:, 1, h:h + 1])
            act(GD[:, h, :], ctab[:, 6, :], AF.Exp, rd, wr, scale=lgcat[:, h:h + 1])
            act(CO[:, h, :], ctab[:, 7, 0:8], AF.Exp, rd, wr, scale=lgcat[:, h:h + 1])
            tt("dve", CO[:, h, :], CO[:, h, :], ctab[:, 8, 0:8], ALU.mult, rd, wr)
        P.dma("sp", RGt[:], rg_in.ap()[:, l, :], reads=[bRGt], writes=[bRGt])

    def phase_A(l, t):
        par = l % 2
        is_p = t >= NT_S
        P.dma("sp", xT[:], Xd.ap()[t], reads=[bXd[t]], writes=bx)
        P.dma("sp", CS[:], cs_in.ap()[t], writes=[bCS])
        ffn(l, "f1", l * 3 + 0)
        if KSUB < 2:
            P.dma("sp", Xe[1].ap()[t], xT[:], reads=bx, writes=[bXe[1][t]])
            return
        rmsnorm_fm(l * 3 + 1, hT, bh)
        P.dma("sp", Xe[1].ap()[t], xT[:], reads=bx, writes=[bXe[1][t]])
        P.dma("sp", Hd[1].ap()[t], hT[:], reads=bh, writes=[bH[1][t]])
        if KSUB < 3:
            return
        for i in range(13):
            s = wget(l, ("winfm", i))
            w = ws[s][:, 0:2048].rearrange("p (a k m) -> p a k m", a=2, k=8)
            pa = (2 * (i % 3))
            pp = pa + 1
            for a, pbx in ((0, pa), (1, pp)):
                for kc in range(8):
                    mm(PB[pbx][:], w[:, a, kc, :], hT[:, kc, :], kc == 0, kc == 7, [bws[s], bh[kc]], [bPB[pbx]], kc == 7)
            wrel(s)
            t1, b1 = tmp()
            t2, b2 = tmp()
            if i < 5:
                gcol = 0 if i < 4 else 2
                sq = PT[i % NPT]
                bsq = bPT[i % NPT]
                act(sq[:], PB[pa][:], AF.Square, [bPB[pa]], [bsq])
                mm(PB[6][:], blk1[:], sq[:], True, True, [bsq, bconst], [bPB[6]], True)
                rs, brs = tmp()
                rsqrt_act(rs[:], PB[6][:], [bPB[6]], [brs])
                stt("dve", t1[:], PB[pa][:], qkg[:, l, gcol:gcol + 1], CS[:, 0, :], ALU.mult, ALU.mult,
                    [bPB[pa], bCS, bconst], [b1])
                stt("dve", t2[:], PB[pp][:], qkg[:, l, gcol + 1:gcol + 2], CS[:, 1, :], ALU.mult, ALU.mult,
                    [bPB[pp], bCS, bconst], [b2])
                tt(PL, t1[:], t1[:], t2[:], ALU.add, [b1, b2], [b1])
                if i < 4:
                    tt(PL, QT[:, i, :], t1[:], rs[:], ALU.mult, [b1, brs], [bQT[i]])
                else:
                    tt(PL, KO[:], t1[:], rs[:], ALU.mult, [b1, brs], [bKO])
            else:
                h = (i - 5) % 4
                isk = i >= 9
                sc = 0.125 if isk else 1.0
                stt("dve", t1[:], PB[pa][:], sc, CS[:, 0, :], ALU.mult, ALU.mult, [bPB[pa], bCS], [b1])
                stt("dve", t2[:], PB[pp][:], sc, CS[:, 1, :], ALU.mult, ALU.mult, [bPB[pp], bCS], [b2])
                if not isk:
                    tt(PL, RQ[:, h, :], t1[:], t2[:], ALU.add, [b1, b2], [bRQs[h]])
                else:
                    tt(PL, t1[:], t1[:], t2[:], ALU.add, [b1, b2], [b1])
                    cp("act", RK[:, h, :], t1[:], [b1], [bRKs[h]])
                    for sub in range(4):
                        P.op("pe", lambda e, t1=t1, sub=sub: e.transpose(
                            PB[7][:, sub * 128:(sub + 1) * 128], t1[:, sub * 128:(sub + 1) * 128], identf[:]),
                            reads=[b1, bconst], writes=[bPB[7]], inc=(sub == 3))
                    src = PB[7][:].rearrange("p (s m) -> p s m", s=4)
                    for sub in range(4):
                        tt("dve", ST[:, sub, h * 128:(h + 1) * 128], src[:, sub, :], KD[:, h, :], ALU.mult,
                           [bPB[7], bDec], [bSTs[sub]])
        if KSUB < 4:
            return
        P.dma("sp", Qd[1].ap()[t], QT[:], reads=bQT, writes=[bQ[1][t]])
        P.dma("sp", RQd[1].ap()[t], RQ[:], reads=bRQs, writes=[bRQ[1][t]])
        P.dma("sp", RKd[1].ap()[t], RK[:], reads=bRKs, writes=[bRK[1][t]])
        if is_p:
            k0 = (t - NT_S) * TT
            P.dma("sp", KTl[par].ap()[:, k0:k0 + TT], KO[:], reads=[bKO], writes=[bKTl[par]])
        else:
            k0 = t * TT
            P.dma("sp", KTs[par].ap()[:, k0:k0 + TT], KO[:], reads=[bKO], writes=[bKTs[par]])
        s = wget(l, ("wrv", 0))
        w = ws[s][:].rearrange("p (k n) -> p k n", k=8)
        for sub in range(4):
            pb = sub % 6
            for kc in range(8):
                mm(PB[pb][:], hT[:, kc, sub * 128:(sub + 1) * 128], w[:, kc, :], kc == 0, kc == 7,
                   [bws[s], bh[kc]], [bPB[pb]], kc == 7)
            cp("act", RV[:, sub, :], PB[pb][:], [bPB[pb]], [bRVs[sub]])
        wrel(s)
        s = wget(l, ("wav", 0))
        w = ws[s][:, 0:1024].rearrange("p (k n) -> p k n", k=8)
        pb = 4
        for sub in range(4):
            for kc in range(8):
                mm(PB[pb][:, sub * 128:(sub + 1) * 128], hT[:, kc, sub * 128:(sub + 1) * 128], w[:, kc, :],
                   kc == 0, kc == 7, [bws[s], bh[kc]], [bPB[pb]], (kc == 7 and sub == 3))
        wrel(s)
        src = PB[pb][:].rearrange("p (s m) -> p s m", s=4)
        cp("dve", VO[:, :, 0:64], src[:, :, 0:64], [bPB[pb]], [bVO])
        cp("dve", VO[:, :, 192:256], src[:, :, 64:128], [bPB[pb]], [bVO])
        P.dma("sp", RVd[1].ap()[t], RV[:], reads=bRVs, writes=[bRV[1][t]])
        if is_p:
            k0 = (t - NT_S) * TT
            dst = VEl[par].ap()[k0:k0 + TT, :].rearrange("(s p) c -> p s c", p=128)
            P.dma("sp", dst, VO[:], reads=[bVO], writes=[bVEl[par]])
        else:
            k0 = t * TT
            dst = VEs[par].ap()[k0:k0 + TT, :].rearrange("(s p) c -> p s c", p=128)
            P.dma("sp", dst, VO[:], reads=[bVO], writes=[bVEs[par]])
        for sub in range(4):
            pb = 5 if sub % 2 == 0 else 6
            for h in range(4):
                mm(PB[pb][:, h * 128:(h + 1) * 128], ST[:, sub, h * 128:(h + 1) * 128], RV[:, sub, h * 128:(h + 1) * 128],
                   True, True, [bSTs[sub], bRVs[sub]], [bPB[pb]], h == 3)
            kv, bkv = tmp()
            cp("act", kv[:], PB[pb][:], [bPB[pb]], [bkv])
            ch = t * 4 + sub
            P.dma("sp", KVd[1].ap()[ch], kv[:], reads=[bkv], writes=[bKV[1][ch]])

    def scan_dir(chunks, fwd, init_from_sin, store, kvpar):
        r0, r1 = (0, 64) if fwd else (64, 128)
        order = chunks if fwd else chunks[::-1]
        if init_from_sin:
            cp("dve", Sacc[r0:r1, :], Sin[r0:r1, :], [bSin, bSacc], [bSacc])
        else:
            P.op("dve", lambda e: e.memset(Sacc[r0:r1, :], 0.0), reads=[bSacc], writes=[bSacc])
        for ch in order:
            if store:
                cp(PL, STo[r0:r1, :], Sacc[r0:r1, :], [bSacc, bSTo], [bSTo])
                P.dma("sp", STd.ap()[ch][r0:r1, :], STo[r0:r1, :], reads=[bSTo], writes=[bST[ch]])
            kv, bkv = tmp()
            P.dma("sp", kv[r0:r1, :], KVd[kvpar].ap()[ch][r0:r1, :], reads=[bKV[kvpar][ch]], writes=[bkv])
            tt("dve", Sacc[r0:r1, :], Sacc[r0:r1, :], GD[r0:r1, :, :].rearrange("p h m -> p (h m)"), ALU.mult,
               [bSacc, bDec], [bSacc])
            tt("dve", Sacc[r0:r1, :], Sacc[r0:r1, :], kv[r0:r1, :], ALU.add, [bSacc, bkv], [bSacc])

    STo = sb("STo", [128, 512], BF16)
    bSTo = Buf()

    def phase_S_prompt(l):
        par = l % 2
        pch = list(range(NT_S * 4, NCH))
        scan_dir(pch, True, False, False, 1)
        scan_dir(pch, False, False, False, 1)
        P.dma("sp", AGi[par].ap()[:, :], Sacc[:], reads=[bSacc], writes=[bAGi[par]])

    def phase_S_finish(l):
        par = l % 2
        pch = list(range(NT_S * 4, NCH))
        sch = list(range(NT_S * 4))
        P.op("dve", lambda e: e.memset(Sin[:], 0.0), reads=[bSin], writes=[bSin])
        for cpr in range(NCORES):
            a, ba = tmp()
            P.dma("sp", a[:], AGo[par].ap()[cpr * 128:(cpr + 1) * 128, :], reads=[bAGo[par]], writes=[ba])
            for h in range(4):
                stt("dve", Sin[:, h * 128:(h + 1) * 128], a[:, h * 128:(h + 1) * 128], CO[:, h, cpr:cpr + 1],
                    Sin[:, h * 128:(h + 1) * 128], ALU.mult, ALU.add, [ba, bSin, bDec], [bSin])
        scan_dir(pch, True, True, True, 0)
        scan_dir(pch, False, True, True, 0)
        scan_dir(sch, True, False, True, 0)
        scan_dir(sch, False, False, True, 0)

    kvn = [0]
    ptn = [0]

    def phase_B(l, t):
        par = l % 2
        is_p = t >= NT_S
        P.dma("sp", xT[:], Xe[0].ap()[t], reads=[bXe[0][t]], writes=bx)
        P.dma("sp", hT[:], Hd[0].ap()[t], reads=[bH[0][t]], writes=bh)
        P.dma("sp", QT[:], Qd[0].ap()[t], reads=[bQ[0][t]], writes=bQT)
        P.dma("sp", RQ[:], RQd[0].ap()[t], reads=[bRQ[0][t]], writes=bRQs)
        P.dma("sp", RK[:], RKd[0].ap()[t], reads=[bRK[0][t]], writes=bRKs)
        P.dma("sp", RV[:], RVd[0].ap()[t], reads=[bRV[0][t]], writes=bRVs)
        for sub in range(4):
            ch = t * 4 + sub
            P.dma("sp", ST[:, sub, :], STd.ap()[ch], reads=[bST[ch]], writes=[bSTs[sub]])
        nkb = (T_P if is_p else T_S) // TT
        for c in range(4):
            oa, ob = 4, 5
            for kb in range(nkb):
                ks = kvn[0] % NKV
                kvn[0] += 1
                if is_p:
                    r = kb // NT_P
                    k0 = (kb % NT_P) * TT
                    ksrc = KTg[par].ap()[r * 128:(r + 1) * 128, k0:k0 + TT]
                    vsrc = VEg[par].ap()[kb * TT:(kb + 1) * TT, :].rearrange("(s p) c -> p s c", p=128)
                    rdk, rdv = bKTg[par], bVEg[par]
                else:
                    ksrc = KTs[par].ap()[:, kb * TT:(kb + 1) * TT]
                    vsrc = VEs[par].ap()[kb * TT:(kb + 1) * TT, :].rearrange("(s p) c -> p s c", p=128)
                    rdk, rdv = bKTs[par], bVEs[par]
                P.dma("sp", KS[ks][:], ksrc, reads=[rdk], writes=[bKVs[ks]])
                P.dma("sp", VS[ks][:], vsrc, reads=[rdv], writes=[bKVsV[ks]])
                for kt in range(4):
                    first = (kb == 0 and kt == 0)
                    last = (kb == nkb - 1 and kt == 3)
                    sa = (2 * (kt % 2))
                    sbk = sa + 1
                    mm(PB[sa][:], KS[ks][0:64, kt * 128:(kt + 1) * 128], QT[0:64, c, :], True, True,
                       [bKVs[ks], bQT[c]], [bPB[sa]], True)
                    mm(PB[sbk][:], KS[ks][64:128, kt * 128:(kt + 1) * 128], QT[64:128, c, :], True, True,
                       [bKVs[ks], bQT[c]], [bPB[sbk]], True)
                    p0 = ptn[0] % NPT
                    p1 = (ptn[0] + 1) % NPT
                    ptn[0] += 2
                    act(PT[p0][:], PB[sa][:], AF.Exp, [bPB[sa]], [bPT[p0]], scale=0.125)
                    act(PT[p1][:], PB[sbk][:], AF.Exp, [bPB[sbk]], [bPT[p1]], scale=0.125)
                    mm(PB[oa][:], VS[ks][:, kt, 0:128], PT[p0][:], first, last, [bKVsV[ks], bPT[p0]], [bPB[oa]], True)
                    mm(PB[ob][:], VS[ks][:, kt, 128:256], PT[p1][:], first, last, [bKVsV[ks], bPT[p1]], [bPB[ob]], True)
            ra, bra = tmp()
            P.op("dve", lambda e, ra=ra: e.reciprocal(out=ra[64:128, :], in_=PB[4][64:128, :]), reads=[bPB[4]], writes=[bra])
            P.op("dve", lambda e, ra=ra: e.reciprocal(out=ra[0:64, :], in_=PB[5][0:64, :]), reads=[bPB[5]], writes=[bra])
            tt("dve", attnT[0:64, c, :], PB[4][0:64, :], ra[64:128, :], ALU.mult, [bPB[4], bra], [battn[c]])
            tt("dve", attnT[64:128, c, :], PB[5][64:128, :], ra[0:64, :], ALU.mult, [bPB[5], bra], [battn[c]])
        s_rg = wget(l, ("wrg", 0))
        wrg = ws[s_rg][:].rearrange("p (k n) -> p k n", k=8)
        for sub in range(4):
            tsl = slice(sub * 128, (sub + 1) * 128)
            psc, po, prg = 0, 1, 2
            for h in range(4):
                mm(PB[psc][:, h * 128:(h + 1) * 128], RK[0:64, h, tsl], RQ[0:64, h, tsl], True, True,
                   [bRKs[h], bRQs[h]], [bPB[psc]], h == 3)
            p0 = ptn[0] % NPT
            ptn[0] += 1
            AT = PT[p0]
            tt("dve", AT[:], PB[psc][:], DT_[:].rearrange("p h m -> p (h m)"), ALU.mult, [bPB[psc], bDec], [bPT[p0]])
            p1 = ptn[0] % NPT
            ptn[0] += 1
            QC = PT[p1]
            tt(PL, QC[:].rearrange("p (h m) -> p h m", h=4), RQ[:, :, tsl], QD[:], ALU.mult,
               list(bRQs) + [bDec], [bPT[p1]])
            for h in range(4):
                hs = slice(h * 128, (h + 1) * 128)
                mm(PB[po][:, hs], AT[:, hs], RV[:, sub, hs], True, False, [bPT[p0], bRVs[sub]], [bPB[po]], False)
                mm(PB[po][:, hs], QC[:, hs], ST[:, sub, hs], False, True, [bPT[p1], bSTs[sub]], [bPB[po]], h == 3)
            for kc in range(8):
                mm(PB[prg][:], hT[:, kc, tsl], wrg[:, kc, :], kc == 0, kc == 7, [bws[s_rg], bh[kc]], [bPB[prg]], kc == 7)
            o3 = PB[po][:].rearrange("p (h m) -> p h m", h=4)
            P.op("dve", lambda e, o3=o3: e.tensor_reduce(out=sm[:, 0:4], in_=o3, axis=AX.X, op=ALU.add),
                 reads=[bPB[po], bsm], writes=[bsm])
            sq, bsq = tmp()
            act(sq[:], PB[po][:], AF.Square, [bPB[po]], [bsq])
            P.op("dve", lambda e, sq=sq: e.tensor_reduce(out=sm[:, 4:8], in_=sq[:].rearrange("p (h m) -> p h m", h=4),
                                                         axis=AX.X, op=ALU.add), reads=[bsq, bsm], writes=[bsm])
            tsm("dve", sm[:, 8:12], sm[:, 0:4], 1.0 / 128, [bsm], [bsm])
            tt("dve", sm[:, 12:16], sm[:, 8:12], sm[:, 8:12], ALU.mult, [bsm], [bsm])
            stt("dve", sm[:, 16:20], sm[:, 4:8], 1.0 / 128, sm[:, 12:16], ALU.mult, ALU.subtract, [bsm], [bsm])
            rsqrt_act(sm[:, 20:24], sm[:, 16:20], [bsm], [bsm])
            stt("dve", sm[:, 24:28], sm[:, 8:12], -1.0, sm[:, 20:24], ALU.mult, ALU.mult, [bsm], [bsm])
            yn, byn = tmp()
            for h in range(4):
                hs = slice(h * 128, (h + 1) * 128)
                ts("dve", yn[:, hs], PB[po][:, hs], sm[:, 20 + h:21 + h], sm[:, 24 + h:25 + h], ALU.mult, ALU.add,
                   [bPB[po], bsm], [byn])
            sg, bsg = tmp()
            act(sg[:], PB[prg][:], AF.Silu, [bPB[prg]], [bsg])
            tt(PL, yn[:], yn[:], RGt[:], ALU.mult, [byn, bRGt], [byn])
            y2, by2 = tmp()
            tt(PL, y2[:], yn[:], sg[:], ALU.mult, [byn, bsg], [by2])
            for h in range(4):
                P.op("pe", lambda e, h=h, y2=y2: e.transpose(
                    PB[7][:, h * 128:(h + 1) * 128], y2[:, h * 128:(h + 1) * 128], identf[:]),
                    reads=[by2, bconst], writes=[bPB[7]], inc=(h == 3))
            cp("act", yrT[:, :, tsl], PB[7][:].rearrange("p (h m) -> p h m", h=4), [bPB[7]], list(byr))
        wrel(s_rg)
        s_ba = wget(l, ("wba", 0))
        s_br = wget(l, ("wbr", 0))
        wba = ws[s_ba][:].rearrange("p (c n) -> p c n", c=4)
        wbr = ws[s_br][:].rearrange("p (c n) -> p c n", c=4)
        for n in range(8):
            ns = slice(n * 128, (n + 1) * 128)
            s_g = wget(l, ("wgate", n))
            wg = ws[s_g][:, 0:2048].rearrange("p (a k m) -> p a k m", a=2, k=8)
            base = 0 if n % 2 == 0 else 3
            pya, pyr, pga = base, base + 1, base + 2
            pgr = 6
            for c in range(4):
                mm(PB[pya][:], wba[:, c, ns], attnT[:, c, :], c == 0, c == 3, [bws[s_ba], battn[c]], [bPB[pya]], c == 3)
            for h in range(4):
                mm(PB[pyr][:], wbr[:, h, ns], yrT[:, h, :], h == 0, h == 3, [bws[s_br], byr[h]], [bPB[pyr]], h == 3)
            for kc in range(8):
                mm(PB[pga][:], wg[:, 0, kc, :], hT[:, kc, :], kc == 0, kc == 7, [bws[s_g], bh[kc]], [bPB[pga]], kc == 7)
            for kc in range(8):
                mm(PB[pgr][:], wg[:, 1, kc, :], hT[:, kc, :], kc == 0, kc == 7, [bws[s_g], bh[kc]], [bPB[pgr]], kc == 7)
            wrel(s_g)
            g1, bg1 = tmp()
            g2, bg2 = tmp()
            act(g1[:], PB[pga][:], AF.Sigmoid, [bPB[pga], bconst], [bg1], bias=bgs[:, l, n:n + 1], scale=1.0)
            act(g2[:], PB[pgr][:], AF.Sigmoid, [bPB[pgr], bconst], [bg2], bias=bgs[:, l, 8 + n:9 + n], scale=1.0)
            tt("dve", g1[:], g1[:], PB[pya][:], ALU.mult, [bg1, bPB[pya]], [bg1])
            tt("dve", g2[:], g2[:], PB[pyr][:], ALU.mult, [bg2, bPB[pyr]], [bg2])
            tt(PL, xn[:, n, :], g1[:], g2[:], ALU.add, [bg1, bg2], [bxn[n]])
        wrel(s_ba)
        wrel(s_br)
        s0 = wget(l, ("wo", 0))
        s1 = wget(l, ("wo", 1))
        w0 = ws[s0][:].rearrange("p (k n) -> p k n", k=4)
        w1 = ws[s1][:].rearrange("p (k n) -> p k n", k=4)
        for n in range(8):
            ns = slice(n * 128, (n + 1) * 128)
            pb = n % 6
            for kc in range(8):
                wsrc, sidx = (w0, s0) if kc < 4 else (w1, s1)
                mm(PB[pb][:], wsrc[:, kc % 4, ns], xn[:, kc, :], kc == 0, kc == 7, [bws[sidx], bxn[kc]], [bPB[pb]], kc == 7)
            tt("dve", xT[:, n, :], xT[:, n, :], PB[pb][:], ALU.add, [bx[n], bPB[pb]], [bx[n]])
        wrel(s0)
        wrel(s1)
        ffn(l, "f2", l * 3 + 2)
        P.dma("sp", Xd.ap()[t], xT[:], reads=bx, writes=[bXd[t]])

    def final_pass(t):
        P.dma("sp", xT[:], Xd.ap()[t], reads=[bXd[t]], writes=bx)
        pb = 6
        for c in range(8):
            sq = PT[c % NPT]
            act(sq[:], xT[:, c, :], AF.Square, [bx[c]], [bPT[c % NPT]])
            mm(PB[pb][:], onesM[:], sq[:], c == 0, c == 7, [bPT[c % NPT], bconst], [bPB[pb]], True)
        rs, brs = tmp()
        rsqrt_act(rs[:], PB[pb][:], [bPB[pb]], [brs])
        for c in range(8):
            stt("dve", xT[:, c, :], xT[:, c, :], nrm[:, DEPTH * 3, c:c + 1], rs[:],
                ALU.mult, ALU.mult, [bx[c], brs, bconst], [bx[c]])
        for sub in range(4):
            xi = xi_n[0] % 2
            xi_n[0] += 1
            for g in range(2):
                pbk = (sub * 2 + g) % 6
                for cc in range(4):
                    c = g * 4 + cc
                    P.op("pe", lambda e, pbk=pbk, cc=cc, c=c, sub=sub: e.transpose(
                        PB[pbk][:, cc * 128:(cc + 1) * 128], xT[:, c, sub * 128:(sub + 1) * 128], identf[:]),
                        reads=[bx[c], bconst], writes=[bPB[pbk]], inc=(cc == 3))
                cp("dve" if g == 0 else "act", XI[xi][:, g * 512:(g + 1) * 512], PB[pbk][:], [bPB[pbk]], [bXI[xi]])
            r0 = t * TT + sub * 128
            by = Buf()
            bYs.append(by)
            P.dma("sp", y_out.ap()[r0:r0 + 128, :], XI[xi][:], reads=[bXI[xi]], writes=[by])

    if RUN_B:
        layer_tables(0)
        phase_S_finish(0)
        for t in order_B:
            phase_B(0, t)
    if RUN_A:
        layer_tables(1)
        for t in order_A:
            phase_A(1, t)
        phase_S_prompt(1)
    if KIND == "last":
        for t in range(NT):
            final_pass(t)
    assert WL.nuse == len(seq), (WL.nuse, len(seq))
    P.final_wait("sp", bYs)

    with nc.Block() as block:
        @block.tensor
        def _(e):
            P.replay(e, "pe")

        @block.scalar
        def _(e):
            P.replay(e, "act")

        @block.vector
        def _(e):
            P.replay(e, "dve")

        @block.gpsimd
        def _(e):
            P.replay(e, "pool")

        @block.sync
        def _(e):
            P.replay(e, "sp")

    stack.close()
    return nc


def run_model(inp, NT_S, NT_P, DEPTH):
    T_S = NT_S * TT
    T_PL = NT_P * TT
    xs = np.asarray(inp["x_sample"], np.float32)
    xp = np.asarray(inp["x_prompt"], np.float32)
    assert xs.shape == (NCORES, T_S, D) and xp.shape == (1, T_PL * NCORES, D)
    pidx = np.arange(128) % 64
    pperm = (pidx + 32) % 64
    ident = np.eye(128, dtype=np.float32)

    def params(lb, la):
        norms = np.empty((128, 7, 8), np.float32)
        qkg = np.empty((128, 2, 4), np.float32)
        bgate = np.empty((128, 2, 16), np.float32)
        retg = np.empty((128, 2, 512), np.float32)
        dec = np.empty((128, 2, 2, 4), np.float32)
        for sl, l in ((0, lb), (1, la)):
            for k, nm in enumerate(("ffn1_norm", "mix_norm", "ffn2_norm")):
                norms[:, sl * 3 + k, :] = np.asarray(inp[nm][l], np.float32).reshape(8, 128).T
            qn = np.asarray(inp["q_norm"][l], np.float32)
            kn = np.asarray(inp["k_norm"][l], np.float32)
            qkg[:, sl, 0] = qn[pidx]
            qkg[:, sl, 1] = qn[pperm]
            qkg[:, sl, 2] = kn[pidx]
            qkg[:, sl, 3] = kn[pperm]
            bgate[:, sl, :] = np.asarray(inp["b_gate"][l], np.float32).reshape(16, 128).T
            retg[:, sl, :] = np.asarray(inp["ret_norm"][l], np.float32)[None, :]
            dec[:, sl, 0, :] = np.asarray(inp["ret_decay_fwd"][l], np.float32)[None, :]
            dec[:, sl, 1, :] = np.asarray(inp["ret_decay_bwd"][l], np.float32)[None, :]
        norms[:, 6, :] = np.asarray(inp["final_norm"], np.float32).reshape(8, 128).T
        return {"norms": norms, "qkg": qkg, "bgate": bgate, "retg": retg, "dec": dec, "ident": ident}

    cs_c, ct_c = [], []
    for c in range(NCORES):
        pos = np.concatenate([np.arange(T_S), c * T_PL + np.arange(T_PL)])
        cs = rope_tables(pos)
        cs_c.append(np.ascontiguousarray(cs.reshape(128, 2, NT_S + NT_P, TT).transpose(2, 0, 1, 3)))
        ct_c.append(const_tables(c, NT_P * 4))

    progs = {}

    def prog(kind):
        if kind not in progs:
            progs[kind] = build_program(NT_S, NT_P, kind)
        return progs[kind]

    STATE = ["Xe", "Hd", "Qd", "RQd", "RKd", "RVd", "KTs", "VEs", "KVd"]
    state = None
    blob_prev = None
    res = None
    for k in range(DEPTH + 1):
        kind = "first" if k == 0 else ("last" if k == DEPTH else "mid")
        lb = max(k - 1, 0)
        la = min(k, DEPTH - 1)
        pr = params(lb, la)
        blob_a = arrange_layer(inp, la) if kind != "last" else None
        in_maps = []
        if state is not None:
            ktg = np.concatenate([state[r]["KTl1"] for r in range(NCORES)], axis=0)
            veg = np.concatenate([state[r]["VEl1"] for r in range(NCORES)], axis=0)
            ago = np.concatenate([state[r]["AGi1"] for r in range(NCORES)], axis=0)
        for c in range(NCORES):
            m = dict(pr)
            m["cs"] = cs_c[c]
            m["ctab"] = ct_c[c]
            if kind == "first":
                m["x_in"] = np.ascontiguousarray(np.concatenate([xs[c], xp[0, c * T_PL:(c + 1) * T_PL]], axis=0))
            else:
                for nm in STATE:
                    m[nm + "0"] = state[c][nm + "1"]
                m["KTg0"] = ktg
                m["VEg0"] = veg
                m["AGo0"] = ago
                m["wf0"] = np.ascontiguousarray(blob_prev[:, FA:])
            if kind != "last":
                m["wf1"] = np.ascontiguousarray(blob_a[:, :FA])
            in_maps.append(m)
        res = run_bass_kernel_spmd(prog(kind), in_maps, core_ids=list(range(NCORES)))
        state = res.results
        blob_prev = blob_a
    ys = np.empty((NCORES, T_S, D), np.float32)
    yp = np.empty((1, T_PL * NCORES, D), np.float32)
    for c in range(NCORES):
        y = np.asarray(res.results[c]["y_out"], np.float32)
        ys[c] = y[:T_S]
        yp[0, c * T_PL:(c + 1) * T_PL] = y[T_S:]
    return yp, ys


def kernel(**inputs):
    return run_model(inputs, NT_S=8, NT_P=4, DEPTH=4)
```

```python
import math
import os
from contextlib import ExitStack

import numpy as np
import ml_dtypes

import concourse.bass as bass
import concourse.mybir as mybir
from concourse.bass_utils import run_bass_kernel_spmd

F32 = mybir.dt.float32
BF16 = mybir.dt.bfloat16
AF = mybir.ActivationFunctionType
ALU = mybir.AluOpType
AX = mybir.AxisListType

NCORES = 8
D = 1024
DFF = 2816
NJ = DFF // 128
HD = 64
C = 128
TT = 512
EPS = 1e-6
GRID_W = 64
ROPE_THETA = 10000.0

WSIZES = [("f1w13", 45056), ("f1w2", 22528), ("winfm", 26624), ("wrv", 4096), ("wav", 1024),
          ("wrg", 4096), ("wgate", 16384), ("wba", 4096), ("wbr", 4096), ("wo", 8192),
          ("f2w13", 45056), ("f2w2", 22528)]
WOFF = {}
_o = 0
for _n, _s in WSIZES:
    WOFF[_n] = _o
    _o += _s
FL = _o
A_NAMES = ["f1w13", "f1w2", "winfm", "wrv", "wav"]
FA = sum(sz for n, sz in WSIZES if n in A_NAMES)
FB = FL - FA
WPART = {n: (1 if n in A_NAMES else 0) for n, _ in WSIZES}
WREL = {n: (WOFF[n] if n in A_NAMES else WOFF[n] - FA) for n, _ in WSIZES}
SLOT = 4096
NSLOT = 5


class Sem:
    def __init__(self, h):
        self.h = h
        self.n = 0


class Buf:
    __slots__ = ("w", "r", "ro", "excl")

    def __init__(self, ro=False, excl=False):
        self.w = {}
        self.r = {}
        self.ro = ro
        self.excl = excl


class Eng:
    def __init__(self, name):
        self.name = name
        self.ops = []
        self.waited = {}
        self.sem = None
        self.pend_r = []
        self.pend_w = []


class Prog:
    def __init__(self, nc, stack):
        self.nc = nc
        self.stack = stack
        self.nsem = 0
        self.E = {n: Eng(n) for n in ["pe", "act", "dve", "pool", "sp"]}
        for n in ["pe", "act", "dve", "pool"]:
            self.E[n].sem = self.newsem()
        self.dsems = {"sp": [self.newsem() for _ in range(28)], "pool": [self.newsem() for _ in range(10)]}
        self.di = {"sp": 0, "pool": 0}
        self.agsems = []

    def newsem(self):
        h = self.stack.enter_context(self.nc.semaphore(f"s{self.nsem}"))
        self.nsem += 1
        return Sem(h)

    def _waits(self, e, reads, writes):
        need = {}
        for b in reads:
            for s, v in b.w.items():
                if need.get(s, 0) < v:
                    need[s] = v
            if b.excl:
                for s, v in b.r.items():
                    if s is not e.sem and need.get(s, 0) < v:
                        need[s] = v
        for b in writes:
            for dct in (b.w, b.r):
                for s, v in dct.items():
                    if need.get(s, 0) < v:
                        need[s] = v
        for s, v in need.items():
            if e.name == "pe" and s is e.sem:
                continue
            if e.waited.get(s, 0) < v:
                e.ops.append(("w", s.h, v))
                e.waited[s] = v

    def op(self, en, fn, reads=(), writes=(), inc=True):
        e = self.E[en]
        self._waits(e, reads, writes)
        if inc:
            if e.sem.n >= 12000:
                e.sem = self.newsem()
            e.sem.n += 1
            s, v = e.sem, e.sem.n
            e.ops.append(("i", fn, s.h))
            for b in list(reads) + e.pend_r:
                if not b.ro:
                    b.r[s] = v
            for b in list(writes) + e.pend_w:
                b.w = {s: v}
                b.r = {}
            e.pend_r = []
            e.pend_w = []
        else:
            e.ops.append(("i", fn, None))
            e.pend_r += [b for b in reads if not b.ro]
            e.pend_w += list(writes)

    def dma(self, q, out, in_, reads=(), writes=()):
        e = self.E[q]
        sems = self.dsems[q]
        s = sems[self.di[q] % len(sems)]
        self.di[q] += 1
        if s.n > 0 and e.waited.get(s, 0) < s.n:
            e.ops.append(("w", s.h, s.n))
            e.waited[s] = s.n
        self._waits(e, reads, writes)
        s.n += 16
        e.ops.append(("d", out, in_, s.h))
        for b in reads:
            if not b.ro:
                b.r[s] = s.n
        for b in writes:
            b.w = {s: s.n}
            b.r = {}

    def allgather(self, in_t, out_t, reads=(), writes=()):
        e = self.E["pool"]
        self._waits(e, reads, writes)
        s = self.newsem()
        s.n = 1
        self.agsems.append(s)
        e.ops.append(("c", in_t, out_t, s.h))
        for b in reads:
            if not b.ro:
                b.r[s] = 1
        for b in writes:
            b.w = {s: 1}
            b.r = {}

    def final_wait(self, en, bufs):
        e = self.E[en]
        self._waits(e, bufs, bufs)
        for q in ("sp", "pool"):
            for s in self.dsems[q]:
                if s.n > 0 and e.waited.get(s, 0) < s.n:
                    e.ops.append(("w", s.h, s.n))
                    e.waited[s] = s.n
        for s in self.agsems:
            e.ops.append(("w", s.h, 1))
        for n in ("pe", "act", "dve", "pool"):
            s = self.E[n].sem
            if s.n > 0:
                e.ops.append(("w", s.h, s.n))

    def replay(self, eng, en):
        for o in self.E[en].ops:
            k = o[0]
            if k == "w":
                eng.wait_ge(o[1], o[2])
            elif k == "i":
                ins = o[1](eng)
                if o[2] is not None:
                    ins.then_inc(o[2], 1)
            elif k == "d":
                eng.dma_start(out=o[1], in_=o[2]).then_inc(o[3], 16)
            elif k == "c":
                eng.collective_compute(
                    "AllGather", ALU.bypass, replica_groups=[list(range(NCORES))],
                    ins=[o[1].ap().opt()], outs=[o[2].ap().opt()],
                ).then_inc(o[3])


def _lhsT_chunks(W, cols_list):
    K = W.shape[0]
    kc = K // 128
    Wr = W.reshape(kc, 128, -1)
    out = np.empty((128, len(cols_list), kc, 128), np.float32)
    for i, cols in enumerate(cols_list):
        out[:, i] = Wr[:, :, cols].transpose(1, 0, 2)
    return out


def _perm64(cols):
    cols = np.asarray(cols)
    return np.concatenate([cols[32:64], cols[0:32]])


def arrange_layer(inp, l):
    ar = np.arange
    parts = {}
    for k, n13, n2 in (("f1", "ffn1_w13", "ffn1_w2"), ("f2", "ffn2_w13", "ffn2_w2")):
        W13 = inp[n13][l]
        cl = []
        for j in range(NJ):
            cl.append(ar(128 * j, 128 * j + 128))
            cl.append(DFF + ar(128 * j, 128 * j + 128))
        parts[k + "w13"] = _lhsT_chunks(W13, cl).reshape(128, -1)
        parts[k + "w2"] = _lhsT_chunks(inp[n2][l], [ar(128 * n, 128 * n + 128) for n in range(8)]).reshape(128, -1)
    Win = inp["w_in"][l]
    cl = []
    for c in range(4):
        h0 = ar(64 * c, 64 * c + 64)
        h1 = ar(64 * (4 + c), 64 * (4 + c) + 64)
        cl.append(np.concatenate([h0, h1]))
        cl.append(np.concatenate([_perm64(h0), _perm64(h1)]))
    k0 = 512 + ar(0, 64)
    k1 = 512 + ar(64, 128)
    cl.append(np.concatenate([k0, k1]))
    cl.append(np.concatenate([_perm64(k0), _perm64(k1)]))
    for base in (768, 1024):
        for h in range(4):
            hc = base + ar(64 * h, 64 * h + 64)
            cl.append(np.concatenate([hc, hc]))
            cl.append(np.concatenate([_perm64(hc), _perm64(hc)]))
    parts["winfm"] = _lhsT_chunks(Win, cl).reshape(128, -1)
    Wr = Win.reshape(8, 128, -1)
    parts["wrv"] = Wr[:, :, 1280:1792].transpose(1, 0, 2).reshape(128, -1)
    parts["wav"] = Wr[:, :, 640:768].transpose(1, 0, 2).reshape(128, -1)
    parts["wrg"] = Wr[:, :, 1792:2304].transpose(1, 0, 2).reshape(128, -1)
    cl = []
    for n in range(8):
        cl.append(ar(128 * n, 128 * n + 128))
        cl.append(1024 + ar(128 * n, 128 * n + 128))
    parts["wgate"] = _lhsT_chunks(inp["w_gate"][l], cl).reshape(128, -1)
    rows = []
    for c in range(4):
        rows.append(ar(64 * c, 64 * c + 64))
        rows.append(ar(64 * (4 + c), 64 * (4 + c) + 64))
    rows = np.concatenate(rows)
    parts["wba"] = inp["w_branch_attn"][l][rows].reshape(4, 128, 1024).transpose(1, 0, 2).reshape(128, -1)
    parts["wbr"] = inp["w_branch_ret"][l].reshape(4, 128, 1024).transpose(1, 0, 2).reshape(128, -1)
    parts["wo"] = inp["w_out"][l].reshape(8, 128, 1024).transpose(1, 0, 2).reshape(128, -1)
    blob = np.empty((128, FL), np.float32)
    for n, s in WSIZES:
        assert parts[n].shape == (128, s), (n, parts[n].shape)
        blob[:, WOFF[n]:WOFF[n] + s] = parts[n]
    return blob


def rope_tables(pos):
    pos = np.asarray(pos)
    row = (pos // GRID_W).astype(np.float32)
    col = (pos % GRID_W).astype(np.float32)
    freqs = (ROPE_THETA ** (-np.arange(16, dtype=np.float32) / 16)).astype(np.float32)
    ang = np.concatenate([row[:, None] * freqs, col[:, None] * freqs], axis=-1)
    cos = np.cos(ang).astype(np.float32).T
    sin = np.sin(ang).astype(np.float32).T
    c64 = np.concatenate([cos, cos], 0)
    s64 = np.concatenate([-sin, sin], 0)
    out = np.empty((128, 2, len(pos)), np.float32)
    out[:, 0] = np.concatenate([c64, c64], 0)
    out[:, 1] = np.concatenate([s64, s64], 0)
    return out


def const_tables(core, nch_p):
    i = np.arange(128, dtype=np.float32)
    jj = i[:, None]
    ii = i[None, :]
    t = np.zeros((128, 9, 128), np.float32)
    t[:, 0] = np.maximum(ii - jj, 0)
    t[:, 1] = (ii >= jj)
    t[:, 2] = np.maximum(jj - ii, 0)
    t[:, 3] = (jj > ii)
    t[0:64, 4] = (ii + 1.0)
    t[64:128, 4] = (C - ii)
    t[:, 5, 0:64] = (C - 1.0 - jj)
    t[:, 5, 64:128] = jj
    t[:, 6] = float(C)
    cp = np.arange(8, dtype=np.float32)
    ef = nch_p * C * (core - 1 - cp)
    mf = (cp < core)
    eb = nch_p * C * (cp - core - 1)
    mb = (cp > core)
    t[0:64, 7, 0:8] = np.where(mf, ef, 0.0)
    t[64:128, 7, 0:8] = np.where(mb, eb, 0.0)
    t[0:64, 8, 0:8] = mf
    t[64:128, 8, 0:8] = mb
    return t


def build_program(NT_S, NT_P, KIND):
    DEPTH = 2
    RUN_B = KIND in ("mid", "last")
    RUN_A = KIND in ("first", "mid")
    NT = NT_S + NT_P
    T_S = NT_S * TT
    T_PL = NT_P * TT
    T_P = T_PL * NCORES
    NCH = NT * 4
    NCH_P = NT_P * 4
    nc = bass.Bass("TRN2", target_bir_lowering=False)
    STAGE = 9
    PL = "dve"
    KSUB = 9
    stack = ExitStack()
    P = Prog(nc, stack)

    def dram(name, shape, dt, kind=None):
        if kind:
            return nc.dram_tensor(name, shape, dt, kind=kind)
        return nc.dram_tensor(name, shape, dt)

    KIN = "ExternalInput" if RUN_B else None
    KOUT = "ExternalOutput" if RUN_A else None

    def pair(name, shape, dt):
        return [dram(name + "0", shape, dt, KIN), dram(name + "1", shape, dt, KOUT)]

    x_in = dram("x_in", [NT * TT, D], F32, "ExternalInput") if KIND == "first" else None
    y_out = dram("y_out", [NT * TT, D], F32, "ExternalOutput") if KIND == "last" else None
    wf = [dram("wf0", [128, FB], F32, "ExternalInput") if RUN_B else None,
          dram("wf1", [128, FA], F32, "ExternalInput") if RUN_A else None]
    cs_in = dram("cs", [NT, 128, 2, TT], F32, "ExternalInput")
    ct_in = dram("ctab", [128, 9, 128], F32, "ExternalInput")
    nrm_in = dram("norms", [128, DEPTH * 3 + 1, 8], F32, "ExternalInput")
    qk_in = dram("qkg", [128, DEPTH, 4], F32, "ExternalInput")
    bg_in = dram("bgate", [128, DEPTH, 16], F32, "ExternalInput")
    rg_in = dram("retg", [128, DEPTH, 512], F32, "ExternalInput")
    dec_in = dram("dec", [128, DEPTH, 2, 4], F32, "ExternalInput")
    id_in = dram("ident", [128, 128], F32, "ExternalInput")
    wb = [dram("wb0", [128, FB], BF16), dram("wb1", [128, FA], BF16)]
    Xd = dram("Xd", [NT, 128, 8, TT], F32)
    Xe = pair("Xe", [NT, 128, 8, TT], F32)
    Hd = pair("Hd", [NT, 128, 8, TT], BF16)
    Qd = pair("Qd", [NT, 128, 4, TT], BF16)
    RQd = pair("RQd", [NT, 128, 4, TT], BF16)
    RKd = pair("RKd", [NT, 128, 4, TT], BF16)
    RVd = pair("RVd", [NT, 128, 4, 512], BF16)
    KTs = pair("KTs", [128, T_S], BF16)
    VEs = pair("VEs", [T_S, 256], BF16)
    KTl = [None, dram("KTl1", [128, T_PL], BF16, KOUT)]
    VEl = [None, dram("VEl1", [T_PL, 256], BF16, KOUT)]
    KTg = [dram("KTg0", [NCORES * 128, T_PL], BF16, KIN), None]
    VEg = [dram("VEg0", [T_P, 256], BF16, KIN), None]
    KVd = pair("KVd", [NCH, 128, 512], F32)
    STd = dram("STd", [NCH, 128, 512], BF16)
    AGi = [None, dram("AGi1", [128, 512], F32, KOUT)]
    AGo = [dram("AGo0", [NCORES * 128, 512], F32, KIN), None]

    bXd = [Buf() for _ in range(NT)]
    bXe = [[Buf() for _ in range(NT)] for _ in range(2)]
    bH = [[Buf() for _ in range(NT)] for _ in range(2)]
    bQ = [[Buf() for _ in range(NT)] for _ in range(2)]
    bRQ = [[Buf() for _ in range(NT)] for _ in range(2)]
    bRK = [[Buf() for _ in range(NT)] for _ in range(2)]
    bRV = [[Buf() for _ in range(NT)] for _ in range(2)]
    bKTs = [Buf() for _ in range(2)]
    bVEs = [Buf() for _ in range(2)]
    bKTl = [Buf() for _ in range(2)]
    bVEl = [Buf() for _ in range(2)]
    bKTg = [Buf() for _ in range(2)]
    bVEg = [Buf() for _ in range(2)]
    bKV = [[Buf() for _ in range(NCH)] for _ in range(2)]
    bST = [Buf() for _ in range(NCH)]
    bAGi = [Buf() for _ in range(2)]
    bAGo = [Buf() for _ in range(2)]
    bWb = [Buf(ro=True) for _ in range(DEPTH)]
    bYs = []

    def sb(name, shape, dt):
        return stack.enter_context(nc.sbuf_tensor(name, shape, dt))

    ws = [sb(f"ws{i}", [128, SLOT], BF16) for i in range(NSLOT)]
    bws = [Buf() for _ in range(NSLOT)]
    xT = sb("xT", [128, 8, TT], F32)
    bx = [Buf() for _ in range(8)]
    hT = sb("hT", [128, 8, TT], BF16)
    bh = [Buf() for _ in range(8)]
    xn = sb("xn", [128, 8, TT], BF16)
    bxn = [Buf() for _ in range(8)]
    gT = sb("gT", [128, NJ, TT], BF16)
    bg = [Buf() for _ in range(NJ)]
    QT = sb("QT", [128, 4, TT], BF16)
    bQT = [Buf() for _ in range(4)]
    RQ = sb("RQ", [128, 4, TT], BF16)
    bRQs = [Buf() for _ in range(4)]
    RK = sb("RK", [128, 4, TT], BF16)
    bRKs = [Buf() for _ in range(4)]
    RV = sb("RV", [128, 4, 512], BF16)
    bRVs = [Buf() for _ in range(4)]
    ST = sb("ST", [128, 4, 512], BF16)
    bSTs = [Buf() for _ in range(4)]
    CS = sb("CS", [128, 2, TT], F32)
    bCS = Buf()
    KO = sb("KO", [128, TT], BF16)
    bKO = Buf()
    VO = sb("VO", [128, 4, 256], BF16)
    bVO = Buf()
    NKV = 3
    KS = [sb(f"KS{i}", [128, TT], BF16) for i in range(NKV)]
    VS = [sb(f"VS{i}", [128, 4, 256], BF16) for i in range(NKV)]
    bKVs = [Buf() for _ in range(NKV)]
    bKVsV = [Buf() for _ in range(NKV)]
    NPT = 4
    PT = [sb(f"PT{i}", [128, TT], BF16) for i in range(NPT)]
    bPT = [Buf() for _ in range(NPT)]
    attnT = sb("attnT", [128, 4, TT], BF16)
    battn = [Buf() for _ in range(4)]
    yrT = sb("yrT", [128, 4, TT], BF16)
    byr = [Buf() for _ in range(4)]
    NTMP = 6
    TM = [sb(f"TM{i}", [128, TT], F32) for i in range(NTMP)]
    bTM = [Buf() for _ in range(NTMP)]
    tmi = [0]

    def tmp():
        i = tmi[0] % NTMP
        tmi[0] += 1
        return TM[i], bTM[i]

    RA = sb("RA", [128, TT], F32)
    bRA = Buf()
    XI = [sb(f"XI{i}", [128, D], F32) for i in range(2)]
    bXI = [Buf() for _ in range(2)]
    DT_ = sb("DTab", [128, 4, 128], F32)
    QD = sb("QDtab", [128, 4, 128], F32)
    KD = sb("KDtab", [128, 4, 128], F32)
    GD = sb("GDtab", [128, 4, 128], F32)
    CO = sb("COtab", [128, 4, 8], F32)
    bDec = Buf()
    RGt = sb("RGt", [128, 512], F32)
    bRGt = Buf()
    ctab = sb("ctab_s", [128, 9, 128], F32)
    nrm = sb("nrm_s", [128, DEPTH * 3 + 1, 8], F32)
    qkg = sb("qkg_s", [128, DEPTH, 4], F32)
    bgs = sb("bg_s", [128, DEPTH, 16], F32)
    decs = sb("dec_s", [128, DEPTH, 2, 4], F32)
    lgall = sb("lgall", [128, 2, 4], F32)
    lgcat = sb("lgcat", [128, 4], F32)
    identf = sb("identf", [128, 128], F32)
    identb = sb("identb", [128, 128], BF16)
    onesM = sb("onesM", [128, 128], BF16)
    blk1 = sb("blk1", [128, 128], BF16)
    epsb = sb("epsb", [128, 1], F32)
    oneb = sb("oneb", [128, 1], F32)
    sm = sb("smalls", [128, 64], F32)
    bsm = Buf()
    Sacc = sb("Sacc", [128, 512], F32)
    bSacc = Buf()
    Sin = sb("Sin", [128, 512], F32)
    bSin = Buf()
    bconst = Buf()

    PB = [stack.enter_context(nc.psum_tensor(f"pb{i}", [128, 512], F32)) for i in range(8)]
    bPB = [Buf(excl=True) for _ in range(8)]

    seq = []

    def seq_ffn(l, k):
        for i in range(11):
            seq.append((l, WREL[k + "w13"] + i * 4096, 4096, (k + "w13", i)))
        for n in range(8):
            seq.append((l, WREL[k + "w2"] + n * 2816, 2816, (k + "w2", n)))

    order_A = list(range(NT_S, NT)) + list(range(NT_S))
    order_B = list(range(NT))
    if RUN_B:
        for t in order_B:
            seq.append((0, WREL["wrg"], 4096, ("wrg", 0)))
            seq.append((0, WREL["wba"], 4096, ("wba", 0)))
            seq.append((0, WREL["wbr"], 4096, ("wbr", 0)))
            for n in range(8):
                seq.append((0, WREL["wgate"] + n * 2048, 2048, ("wgate", n)))
            seq.append((0, WREL["wo"], 4096, ("wo", 0)))
            seq.append((0, WREL["wo"] + 4096, 4096, ("wo", 1)))
            seq_ffn(0, "f2")
    if RUN_A:
        for t in order_A:
            seq_ffn(1, "f1")
            for i in range(13):
                seq.append((1, WREL["winfm"] + i * 2048, 2048, ("winfm", i)))
            seq.append((1, WREL["wrv"], 4096, ("wrv", 0)))
            seq.append((1, WREL["wav"], 1024, ("wav", 0)))

    class WL:
        free = list(range(NSLOT))
        nload = 0
        nuse = 0
        slot_of = {}

    def wl_topup():
        while WL.free and WL.nload < len(seq) and WL.nload < WL.nuse + NSLOT:
            l, off, size, key = seq[WL.nload]
            s = WL.free.pop(0)
            P.dma("sp", ws[s][:, 0:size], wb[l].ap()[:, off:off + size], reads=[bWb[l]], writes=[bws[s]])
            WL.slot_of[WL.nload] = s
            WL.nload += 1

    def wget(l, key):
        i = WL.nuse
        while not (seq[i][0] == l and seq[i][3] == key):
            assert STAGE < 9 or KSUB < 9, (seq[i], l, key)
            if i >= WL.nload:
                wl_topup()
            s_ = WL.slot_of.pop(i)
            WL.nuse += 1
            WL.free.append(s_)
            i = WL.nuse
        if i >= WL.nload:
            wl_topup()
        assert i < WL.nload, "weight loader deadlock"
        WL.nuse += 1
        s = WL.slot_of.pop(i)
        return s

    def wrel(s):
        WL.free.append(s)
        wl_topup()

    def mm(out, lhsT, rhs, start, stop, reads, writes, inc):
        P.op("pe", lambda e: e.matmul(out, lhsT=lhsT, rhs=rhs, start=start, stop=stop),
             reads=reads, writes=writes, inc=inc)

    def act(out, in_, func, reads, writes, bias=None, scale=None):
        kw = {}
        if bias is not None:
            kw["bias"] = bias
        if scale is not None:
            kw["scale"] = scale
        P.op("act", lambda e: e.activation(out=out, in_=in_, func=func, **kw), reads=reads, writes=writes)

    def tt(en, out, in0, in1, op, reads, writes):
        P.op(en, lambda e: e.tensor_tensor(out=out, in0=in0, in1=in1, op=op), reads=reads, writes=writes)

    def stt(en, out, in0, scalar, in1, op0, op1, reads, writes):
        P.op(en, lambda e: e.scalar_tensor_tensor(out=out, in0=in0, scalar=scalar, in1=in1, op0=op0, op1=op1),
             reads=reads, writes=writes)

    def ts(en, out, in0, s1, s2, op0, op1, reads, writes):
        P.op(en, lambda e: e.tensor_scalar(out=out, in0=in0, scalar1=s1, scalar2=s2, op0=op0, op1=op1),
             reads=reads, writes=writes)

    def cp(en, out, in_, reads, writes):
        if en == "act":
            P.op(en, lambda e: e.activation(out=out, in_=in_, func=AF.Copy), reads=reads, writes=writes)
        else:
            P.op(en, lambda e: e.tensor_copy(out=out, in_=in_), reads=reads, writes=writes)

    def tsm(en, out, in0, s1, reads, writes):
        P.op(en, lambda e: e.tensor_scalar_mul(out=out, in0=in0, scalar1=s1), reads=reads, writes=writes)

    def rsqrt_act(out, in_, reads, writes, nrows=128):
        act(out, in_, AF.Ln, reads, writes, bias=epsb[0:nrows, 0:1], scale=1.0)
        act(out, out, AF.Exp, writes, writes, scale=-0.5)

    for l in range(DEPTH):
        if wf[l] is None:
            continue
        for r in range(8):
            P.dma("pool", wb[l].ap()[16 * r:16 * r + 16, :], wf[l].ap()[16 * r:16 * r + 16, :], writes=[bWb[l]])
    P.dma("sp", ctab[:], ct_in.ap()[:, :, :], writes=[bconst])
    P.dma("sp", nrm[:], nrm_in.ap()[:, :, :], writes=[bconst])
    P.dma("sp", qkg[:], qk_in.ap()[:, :, :], writes=[bconst])
    P.dma("sp", bgs[:], bg_in.ap()[:, :, :], writes=[bconst])
    P.dma("sp", decs[:], dec_in.ap()[:, :, :, :], writes=[bconst])
    P.dma("sp", identf[:], id_in.ap()[:, :], writes=[bconst])
    P.op("dve", lambda e: e.tensor_copy(out=identb[:], in_=identf[:]), reads=[bconst], writes=[bconst])
    P.op("dve", lambda e: e.memset(onesM[:], 1.0 / D), writes=[bconst])
    P.op("dve", lambda e: e.memset(blk1[:], 0.0), writes=[bconst])
    P.op("dve", lambda e: e.memset(blk1[0:64, 0:64], 1.0 / HD), writes=[bconst])
    P.op("dve", lambda e: e.memset(blk1[64:128, 64:128], 1.0 / HD), writes=[bconst])
    P.op("dve", lambda e: e.memset(epsb[:], EPS), writes=[bconst])
    P.op("dve", lambda e: e.memset(oneb[:], 1.0), writes=[bconst])
    P.op("dve", lambda e: e.memset(VO[:], 1.0), writes=[bVO])

    xi_n = [0]
    for t in (range(NT) if KIND == "first" else []):
        for sub in range(4):
            xi = xi_n[0] % 2
            xi_n[0] += 1
            r0 = t * TT + sub * 128
            P.dma("sp", XI[xi][:], x_in.ap()[r0:r0 + 128, :], writes=[bXI[xi]])
            for g in range(2):
                pb = (sub * 2 + g) % 7
                for cc in range(4):
                    c = g * 4 + cc
                    P.op("pe", lambda e, pb=pb, cc=cc, c=c, xi=xi: e.transpose(
                        PB[pb][:, cc * 128:(cc + 1) * 128], XI[xi][:, c * 128:(c + 1) * 128], identf[:]),
                        reads=[bXI[xi], bconst], writes=[bPB[pb]], inc=(cc == 3))
                src = PB[pb][:].rearrange("p (c t) -> p c t", c=4)
                cp("dve" if g == 0 else "act", xT[:, g * 4:(g + 1) * 4, sub * 128:(sub + 1) * 128], src,
                   reads=[bPB[pb]], writes=bx[g * 4:(g + 1) * 4])
        P.dma("sp", Xd.ap()[t], xT[:], reads=bx, writes=[bXd[t]])

    def rmsnorm_fm(gidx, dst, bdst, dst_f32=False):
        pb = 6
        for c in range(8):
            sq = PT[c % NPT]
            act(sq[:], xT[:, c, :], AF.Square, [bx[c]], [bPT[c % NPT]])
            mm(PB[pb][:], onesM[:], sq[:], c == 0, c == 7, [bPT[c % NPT], bconst], [bPB[pb]], True)
        rs, brs = tmp()
        rsqrt_act(rs[:], PB[pb][:], [bPB[pb]], [brs])
        for c in range(8):
            stt("dve", dst[:, c, :], xT[:, c, :], nrm[:, gidx, c:c + 1], rs[:], ALU.mult, ALU.mult,
                [bx[c], brs, bconst], [bdst[c]])

    def ffn(l, k, gidx):
        rmsnorm_fm(gidx, xn, bxn)
        for i in range(11):
            s = wget(l, (k + "w13", i))
            w = ws[s][:].rearrange("p (j a k m) -> p j a k m", j=2, a=2, k=8)
            for jj in range(2):
                j = 2 * i + jj
                pa = (2 * (j % 3)) % 7
                pbk = pa + 1
                for a, pbx in ((0, pa), (1, pbk)):
                    for kc in range(8):
                        mm(PB[pbx][:], w[:, jj, a, kc, :], xn[:, kc, :], kc == 0, kc == 7,
                           [bws[s], bxn[kc]], [bPB[pbx]], kc == 7)
                sa, bsa = tmp()
                act(sa[:], PB[pa][:], AF.Silu, [bPB[pa]], [bsa])
                tt("dve", gT[:, j, :], sa[:], PB[pbk][:], ALU.mult, [bsa, bPB[pbk]], [bg[j]])
            wrel(s)
        for n in range(8):
            s = wget(l, (k + "w2", n))
            w = ws[s][:, 0:2816].rearrange("p (j m) -> p j m", j=NJ)
            pb = n % 6
            for j in range(NJ):
                mm(PB[pb][:], w[:, j, :], gT[:, j, :], j == 0, j == NJ - 1, [bws[s], bg[j]], [bPB[pb]], j == NJ - 1)
            wrel(s)
            stt("dve", xT[:, n, :], PB[pb][:], 0.5, xT[:, n, :], ALU.mult, ALU.add, [bPB[pb], bx[n]], [bx[n]])

    def layer_tables(l):
        rd = [bconst, bDec]
        wr = [bDec]
        act(lgall[:], decs[:, l, :, :], AF.Exp, rd, wr)
        act(lgall[:], lgall[:], AF.Ln, rd, wr, bias=oneb[:, 0:1], scale=-1.0)
        cp("dve", lgcat[0:64, :], lgall[0:64, 0, :], rd, wr)
        cp("dve", lgcat[64:128, :], lgall[64:128, 1, :], rd, wr)
        for h in range(4):
            t1, b1 = tmp()
            t2, b2 = tmp()
            act(t1[:, 0:128], ctab[:, 0, :], AF.Exp, rd, [b1], scale=lgall[:, 0, h:h + 1])
            tt("dve", t1[:, 0:128], t1[:, 0:128], ctab[:, 1, :], ALU.mult, [b1, bconst], [b1])
            act(t2[:, 0:128], ctab[:, 2, :], AF.Exp, rd, [b2], scale=lgall[:, 1, h:h + 1])
            tt("dve", t2[:, 0:128], t2[:, 0:128], ctab[:, 3, :], ALU.mult, [b2, bconst], [b2])
            tt("dve", DT_[:, h, :], t1[:, 0:128], t2[:, 0:128], ALU.add, [b1, b2] + rd, wr)
            act(QD[:, h, :], ctab[:, 4, :], AF.Exp, rd, wr, scale=lgcat[:, h:h + 1])
            act(KD[:, h, 0:64], ctab[:, 5, 0:64], AF.Exp, rd, wr, scale=lgall[:, 0, h:h + 1])
            act(KD[:, h, 64:128], ctab[:, 5, 64:128], AF.Exp, rd, wr, scale=lgall[:, 1, h:h + 1])
            act(GD[:, h, :], ctab[:, 6, :], AF.Exp, rd, wr, scale=lgcat[:, h:h + 1])
            act(CO[:, h, :], ctab[:, 7, 0:8], AF.Exp, rd, wr, scale=lgcat[:, h:h + 1])
            tt("dve", CO[:, h, :], CO[:, h, :], ctab[:, 8, 0:8], ALU.mult, rd, wr)
        P.dma("sp", RGt[:], rg_in.ap()[:, l, :], reads=[bRGt], writes=[bRGt])

    def phase_A(l, t):
        par = l % 2
        is_p = t >= NT_S
        P.dma("sp", xT[:], Xd.ap()[t], reads=[bXd[t]], writes=bx)
        P.dma("sp", CS[:], cs_in.ap()[t], writes=[bCS])
        ffn(l, "f1", l * 3 + 0)
        if KSUB < 2:
            P.dma("sp", Xe[1].ap()[t], xT[:], reads=bx, writes=[bXe[1][t]])
            return
        rmsnorm_fm(l * 3 + 1, hT, bh)
        P.dma("sp", Xe[1].ap()[t], xT[:], reads=bx, writes=[bXe[1][t]])
        P.dma("sp", Hd[1].ap()[t], hT[:], reads=bh, writes=[bH[1][t]])
        if KSUB < 3:
            return
        for i in range(13):
            s = wget(l, ("winfm", i))
            w = ws[s][:, 0:2048].rearrange("p (a k m) -> p a k m", a=2, k=8)
            pa = (2 * (i % 3))
            pp = pa + 1
            for a, pbx in ((0, pa), (1, pp)):
                for kc in range(8):
                    mm(PB[pbx][:], w[:, a, kc, :], hT[:, kc, :], kc == 0, kc == 7, [bws[s], bh[kc]], [bPB[pbx]], kc == 7)
            wrel(s)
            t1, b1 = tmp()
            t2, b2 = tmp()
            if i < 5:
                gcol = 0 if i < 4 else 2
                sq = PT[i % NPT]
                bsq = bPT[i % NPT]
                act(sq[:], PB[pa][:], AF.Square, [bPB[pa]], [bsq])
                mm(PB[6][:], blk1[:], sq[:], True, True, [bsq, bconst], [bPB[6]], True)
                rs, brs = tmp()
                rsqrt_act(rs[:], PB[6][:], [bPB[6]], [brs])
                stt("dve", t1[:], PB[pa][:], qkg[:, l, gcol:gcol + 1], CS[:, 0, :], ALU.mult, ALU.mult,
                    [bPB[pa], bCS, bconst], [b1])
                stt("dve", t2[:], PB[pp][:], qkg[:, l, gcol + 1:gcol + 2], CS[:, 1, :], ALU.mult, ALU.mult,
                    [bPB[pp], bCS, bconst], [b2])
                tt(PL, t1[:], t1[:], t2[:], ALU.add, [b1, b2], [b1])
                if i < 4:
                    tt(PL, QT[:, i, :], t1[:], rs[:], ALU.mult, [b1, brs], [bQT[i]])
                else:
                    tt(PL, KO[:], t1[:], rs[:], ALU.mult, [b1, brs], [bKO])
            else:
                h = (i - 5) % 4
                isk = i >= 9
                sc = 0.125 if isk else 1.0
                stt("dve", t1[:], PB[pa][:], sc, CS[:, 0, :], ALU.mult, ALU.mult, [bPB[pa], bCS], [b1])
                stt("dve", t2[:], PB[pp][:], sc, CS[:, 1, :], ALU.mult, ALU.mult, [bPB[pp], bCS], [b2])
                if not isk:
                    tt(PL, RQ[:, h, :], t1[:], t2[:], ALU.add, [b1, b2], [bRQs[h]])
                else:
                    tt(PL, t1[:], t1[:], t2[:], ALU.add, [b1, b2], [b1])
                    cp("act", RK[:, h, :], t1[:], [b1], [bRKs[h]])
                    for sub in range(4):
                        P.op("pe", lambda e, t1=t1, sub=sub: e.transpose(
                            PB[7][:, sub * 128:(sub + 1) * 128], t1[:, sub * 128:(sub + 1) * 128], identf[:]),
                            reads=[b1, bconst], writes=[bPB[7]], inc=(sub == 3))
                    src = PB[7][:].rearrange("p (s m) -> p s m", s=4)
                    for sub in range(4):
                        tt("dve", ST[:, sub, h * 128:(h + 1) * 128], src[:, sub, :], KD[:, h, :], ALU.mult,
                           [bPB[7], bDec], [bSTs[sub]])
        if KSUB < 4:
            return
        P.dma("sp", Qd[1].ap()[t], QT[:], reads=bQT, writes=[bQ[1][t]])
        P.dma("sp", RQd[1].ap()[t], RQ[:], reads=bRQs, writes=[bRQ[1][t]])
        P.dma("sp", RKd[1].ap()[t], RK[:], reads=bRKs, writes=[bRK[1][t]])
        if is_p:
            k0 = (t - NT_S) * TT
            P.dma("sp", KTl[par].ap()[:, k0:k0 + TT], KO[:], reads=[bKO], writes=[bKTl[par]])
        else:
            k0 = t * TT
            P.dma("sp", KTs[par].ap()[:, k0:k0 + TT], KO[:], reads=[bKO], writes=[bKTs[par]])
        s = wget(l, ("wrv", 0))
        w = ws[s][:].rearrange("p (k n) -> p k n", k=8)
        for sub in range(4):
            pb = sub % 6
            for kc in range(8):
                mm(PB[pb][:], hT[:, kc, sub * 128:(sub + 1) * 128], w[:, kc, :], kc == 0, kc == 7,
                   [bws[s], bh[kc]], [bPB[pb]], kc == 7)
            cp("act", RV[:, sub, :], PB[pb][:], [bPB[pb]], [bRVs[sub]])
        wrel(s)
        s = wget(l, ("wav", 0))
        w = ws[s][:, 0:1024].rearrange("p (k n) -> p k n", k=8)
        pb = 4
        for sub in range(4):
            for kc in range(8):
                mm(PB[pb][:, sub * 128:(sub + 1) * 128], hT[:, kc, sub * 128:(sub + 1) * 128], w[:, kc, :],
                   kc == 0, kc == 7, [bws[s], bh[kc]], [bPB[pb]], (kc == 7 and sub == 3))
        wrel(s)
        src = PB[pb][:].rearrange("p (s m) -> p s m", s=4)
        cp("dve", VO[:, :, 0:64], src[:, :, 0:64], [bPB[pb]], [bVO])
        cp("dve", VO[:, :, 192:256], src[:, :, 64:128], [bPB[pb]], [bVO])
        P.dma("sp", RVd[1].ap()[t], RV[:], reads=bRVs, writes=[bRV[1][t]])
        if is_p:
            k0 = (t - NT_S) * TT
            dst = VEl[par].ap()[k0:k0 + TT, :].rearrange("(s p) c -> p s c", p=128)
            P.dma("sp", dst, VO[:], reads=[bVO], writes=[bVEl[par]])
        else:
            k0 = t * TT
            dst = VEs[par].ap()[k0:k0 + TT, :].rearrange("(s p) c -> p s c", p=128)
            P.dma("sp", dst, VO[:], reads=[bVO], writes=[bVEs[par]])
        for sub in range(4):
            pb = 5 if sub % 2 == 0 else 6
            for h in range(4):
                mm(PB[pb][:, h * 128:(h + 1) * 128], ST[:, sub, h * 128:(h + 1) * 128], RV[:, sub, h * 128:(h + 1) * 128],
                   True, True, [bSTs[sub], bRVs[sub]], [bPB[pb]], h == 3)
            kv, bkv = tmp()
            cp("act", kv[:], PB[pb][:], [bPB[pb]], [bkv])
            ch = t * 4 + sub
            P.dma("sp", KVd[1].ap()[ch], kv[:], reads=[bkv], writes=[bKV[1][ch]])

    def scan_dir(chunks, fwd, init_from_sin, store, kvpar):
        r0, r1 = (0, 64) if fwd else (64, 128)
        order = chunks if fwd else chunks[::-1]
        if init_from_sin:
            cp("dve", Sacc[r0:r1, :], Sin[r0:r1, :], [bSin, bSacc], [bSacc])
        else:
            P.op("dve", lambda e: e.memset(Sacc[r0:r1, :], 0.0), reads=[bSacc], writes=[bSacc])
        for ch in order:
            if store:
                cp(PL, STo[r0:r1, :], Sacc[r0:r1, :], [bSacc, bSTo], [bSTo])
                P.dma("sp", STd.ap()[ch][r0:r1, :], STo[r0:r1, :], reads=[bSTo], writes=[bST[ch]])
            kv, bkv = tmp()
            P.dma("sp", kv[r0:r1, :], KVd[kvpar].ap()[ch][r0:r1, :], reads=[bKV[kvpar][ch]], writes=[bkv])
            tt("dve", Sacc[r0:r1, :], Sacc[r0:r1, :], GD[r0:r1, :, :].rearrange("p h m -> p (h m)"), ALU.mult,
               [bSacc, bDec], [bSacc])
            tt("dve", Sacc[r0:r1, :], Sacc[r0:r1, :], kv[r0:r1, :], ALU.add, [bSacc, bkv], [bSacc])

    STo = sb("STo", [128, 512], BF16)
    bSTo = Buf()

    def phase_S_prompt(l):
        par = l % 2
        pch = list(range(NT_S * 4, NCH))
        scan_dir(pch, True, False, False, 1)
        scan_dir(pch, False, False, False, 1)
        P.dma("sp", AGi[par].ap()[:, :], Sacc[:], reads=[bSacc], writes=[bAGi[par]])

    def phase_S_finish(l):
        par = l % 2
        pch = list(range(NT_S * 4, NCH))
        sch = list(range(NT_S * 4))
        P.op("dve", lambda e: e.memset(Sin[:], 0.0), reads=[bSin], writes=[bSin])
        for cpr in range(NCORES):
            a, ba = tmp()
            P.dma("sp", a[:], AGo[par].ap()[cpr * 128:(cpr + 1) * 128, :], reads=[bAGo[par]], writes=[ba])
            for h in range(4):
                stt("dve", Sin[:, h * 128:(h + 1) * 128], a[:, h * 128:(h + 1) * 128], CO[:, h, cpr:cpr + 1],
                    Sin[:, h * 128:(h + 1) * 128], ALU.mult, ALU.add, [ba, bSin, bDec], [bSin])
        scan_dir(pch, True, True, True, 0)
        scan_dir(pch, False, True, True, 0)
        scan_dir(sch, True, False, True, 0)
        scan_dir(sch, False, False, True, 0)

    kvn = [0]
    ptn = [0]

    def phase_B(l, t):
        par = l % 2
        is_p = t >= NT_S
        P.dma("sp", xT[:], Xe[0].ap()[t], reads=[bXe[0][t]], writes=bx)
        P.dma("sp", hT[:], Hd[0].ap()[t], reads=[bH[0][t]], writes=bh)
        P.dma("sp", QT[:], Qd[0].ap()[t], reads=[bQ[0][t]], writes=bQT)
        P.dma("sp", RQ[:], RQd[0].ap()[t], reads=[bRQ[0][t]], writes=bRQs)
        P.dma("sp", RK[:], RKd[0].ap()[t], reads=[bRK[0][t]], writes=bRKs)
        P.dma("sp", RV[:], RVd[0].ap()[t], reads=[bRV[0][t]], writes=bRVs)
        for sub in range(4):
            ch = t * 4 + sub
            P.dma("sp", ST[:, sub, :], STd.ap()[ch], reads=[bST[ch]], writes=[bSTs[sub]])
        s_rg = wget(l, ("wrg", 0))
        wrg = ws[s_rg][:].rearrange("p (k n) -> p k n", k=8)
        AT, bAT = KO, bKO
        QC, bQC = STo, bSTo
        r_sq, r_yn, b_sqyn = XI[0][:, 0:512], XI[0][:, 512:1024], bXI[0]
        r_sg, r_y2, b_sgy2 = XI[1][:, 0:512], XI[1][:, 512:1024], bXI[1]

        def ret_stages():
            pso, prg = 6, 7
            for sub in range(4):
                tsl = slice(sub * 128, (sub + 1) * 128)
                for kc in range(8):
                    mm(PB[prg][:], hT[:, kc, tsl], wrg[:, kc, :], kc == 0, kc == 7, [bws[s_rg], bh[kc]], [bPB[prg]], kc == 7)
                yield
                act(r_sg, PB[prg][:], AF.Silu, [bPB[prg]], [b_sgy2])
                yield
                for h in range(4):
                    mm(PB[pso][:, h * 128:(h + 1) * 128], RK[0:64, h, tsl], RQ[0:64, h, tsl], True, True,
                       [bRKs[h], bRQs[h]], [bPB[pso]], h == 3)
                tt("dve", QC[:].rearrange("p (h m) -> p h m", h=4), RQ[:, :, tsl], QD[:], ALU.mult,
                   list(bRQs) + [bDec], [bQC])
                yield
                tt("dve", AT[:], PB[pso][:], DT_[:].rearrange("p h m -> p (h m)"), ALU.mult, [bPB[pso], bDec], [bAT])
                yield
                for h in range(4):
                    hs = slice(h * 128, (h + 1) * 128)
                    mm(PB[pso][:, hs], AT[:, hs], RV[:, sub, hs], True, False, [bAT, bRVs[sub]], [bPB[pso]], False)
                    mm(PB[pso][:, hs], QC[:, hs], ST[:, sub, hs], False, True, [bQC, bSTs[sub]], [bPB[pso]], h == 3)
                yield
                o3 = PB[pso][:].rearrange("p (h m) -> p h m", h=4)
                P.op("dve", lambda e, o3=o3: e.tensor_reduce(out=sm[:, 0:4], in_=o3, axis=AX.X, op=ALU.add),
                     reads=[bPB[pso], bsm], writes=[bsm])
                act(r_sq, PB[pso][:], AF.Square, [bPB[pso]], [b_sqyn])
                yield
                P.op("dve", lambda e: e.tensor_reduce(out=sm[:, 4:8], in_=r_sq.rearrange("p (h m) -> p h m", h=4),
                                                      axis=AX.X, op=ALU.add), reads=[b_sqyn, bsm], writes=[bsm])
                tsm("dve", sm[:, 8:12], sm[:, 0:4], 1.0 / 128, [bsm], [bsm])
                tt("dve", sm[:, 12:16], sm[:, 8:12], sm[:, 8:12], ALU.mult, [bsm], [bsm])
                stt("dve", sm[:, 16:20], sm[:, 4:8], 1.0 / 128, sm[:, 12:16], ALU.mult, ALU.subtract, [bsm], [bsm])
                yield
                rsqrt_act(sm[:, 20:24], sm[:, 16:20], [bsm], [bsm])
                yield
                stt("dve", sm[:, 24:28], sm[:, 8:12], -1.0, sm[:, 20:24], ALU.mult, ALU.mult, [bsm], [bsm])
                for h in range(4):
                    hs = slice(h * 128, (h + 1) * 128)
                    ts("dve", r_yn[:, hs], PB[pso][:, hs], sm[:, 20 + h:21 + h], sm[:, 24 + h:25 + h], ALU.mult, ALU.add,
                       [bPB[pso], bsm], [b_sqyn])
                yield
                tt("dve", r_yn, r_yn, RGt[:], ALU.mult, [b_sqyn, bRGt], [b_sqyn])
                tt("dve", r_y2, r_yn, r_sg, ALU.mult, [b_sqyn, b_sgy2], [b_sgy2])
                yield
                for h in range(4):
                    P.op("pe", lambda e, h=h: e.transpose(
                        PB[prg][:, h * 128:(h + 1) * 128], r_y2[:, h * 128:(h + 1) * 128], identf[:]),
                        reads=[b_sgy2, bconst], writes=[bPB[prg]], inc=(h == 3))
                yield
                cp("act", yrT[:, :, tsl], PB[prg][:].rearrange("p (h m) -> p h m", h=4), [bPB[prg]], list(byr))
                yield

        ret_gen = ret_stages()
        unit_n = [0]
        nkb = (T_P if is_p else T_S) // TT
        for c in range(4):
            oa, ob = 4, 5
            for kb in range(nkb):
                ks = kvn[0] % NKV
                kvn[0] += 1
                if is_p:
                    r = kb // NT_P
                    k0 = (kb % NT_P) * TT
                    ksrc = KTg[par].ap()[r * 128:(r + 1) * 128, k0:k0 + TT]
                    vsrc = VEg[par].ap()[kb * TT:(kb + 1) * TT, :].rearrange("(s p) c -> p s c", p=128)
                    rdk, rdv = bKTg[par], bVEg[par]
                else:
                    ksrc = KTs[par].ap()[:, kb * TT:(kb + 1) * TT]
                    vsrc = VEs[par].ap()[kb * TT:(kb + 1) * TT, :].rearrange("(s p) c -> p s c", p=128)
                    rdk, rdv = bKTs[par], bVEs[par]
                P.dma("sp", KS[ks][:], ksrc, reads=[rdk], writes=[bKVs[ks]])
                P.dma("sp", VS[ks][:], vsrc, reads=[rdv], writes=[bKVsV[ks]])
                for kt in range(4):
                    first = (kb == 0 and kt == 0)
                    last = (kb == nkb - 1 and kt == 3)
                    sa = (2 * (kt % 2))
                    sbk = sa + 1
                    mm(PB[sa][:], KS[ks][0:64, kt * 128:(kt + 1) * 128], QT[0:64, c, :], True, True,
                       [bKVs[ks], bQT[c]], [bPB[sa]], True)
                    mm(PB[sbk][:], KS[ks][64:128, kt * 128:(kt + 1) * 128], QT[64:128, c, :], True, True,
                       [bKVs[ks], bQT[c]], [bPB[sbk]], True)
                    p0 = ptn[0] % NPT
                    p1 = (ptn[0] + 1) % NPT
                    ptn[0] += 2
                    act(PT[p0][:], PB[sa][:], AF.Exp, [bPB[sa]], [bPT[p0]], scale=0.125)
                    act(PT[p1][:], PB[sbk][:], AF.Exp, [bPB[sbk]], [bPT[p1]], scale=0.125)
                    mm(PB[oa][:], VS[ks][:, kt, 0:128], PT[p0][:], first, last, [bKVsV[ks], bPT[p0]], [bPB[oa]], True)
                    mm(PB[ob][:], VS[ks][:, kt, 128:256], PT[p1][:], first, last, [bKVsV[ks], bPT[p1]], [bPB[ob]], True)
                    unit_n[0] += 1
                    if unit_n[0] % 2 == 0:
                        next(ret_gen, None)
            ra, bra = RA, bRA
            P.op("dve", lambda e, ra=ra: e.reciprocal(out=ra[64:128, :], in_=PB[4][64:128, :]), reads=[bPB[4]], writes=[bra])
            P.op("dve", lambda e, ra=ra: e.reciprocal(out=ra[0:64, :], in_=PB[5][0:64, :]), reads=[bPB[5]], writes=[bra])
            tt("dve", attnT[0:64, c, :], PB[4][0:64, :], ra[64:128, :], ALU.mult, [bPB[4], bra], [battn[c]])
            tt("dve", attnT[64:128, c, :], PB[5][64:128, :], ra[0:64, :], ALU.mult, [bPB[5], bra], [battn[c]])
        for _ in ret_gen:
            pass
        wrel(s_rg)
        s_ba = wget(l, ("wba", 0))
        s_br = wget(l, ("wbr", 0))
        wba = ws[s_ba][:].rearrange("p (c n) -> p c n", c=4)
        wbr = ws[s_br][:].rearrange("p (c n) -> p c n", c=4)
        for n in range(8):
            ns = slice(n * 128, (n + 1) * 128)
            s_g = wget(l, ("wgate", n))
            wg = ws[s_g][:, 0:2048].rearrange("p (a k m) -> p a k m", a=2, k=8)
            base = 0 if n % 2 == 0 else 3
            pya, pyr, pga = base, base + 1, base + 2
            pgr = 6
            for c in range(4):
                mm(PB[pya][:], wba[:, c, ns], attnT[:, c, :], c == 0, c == 3, [bws[s_ba], battn[c]], [bPB[pya]], c == 3)
            for h in range(4):
                mm(PB[pyr][:], wbr[:, h, ns], yrT[:, h, :], h == 0, h == 3, [bws[s_br], byr[h]], [bPB[pyr]], h == 3)
            for kc in range(8):
                mm(PB[pga][:], wg[:, 0, kc, :], hT[:, kc, :], kc == 0, kc == 7, [bws[s_g], bh[kc]], [bPB[pga]], kc == 7)
            for kc in range(8):
                mm(PB[pgr][:], wg[:, 1, kc, :], hT[:, kc, :], kc == 0, kc == 7, [bws[s_g], bh[kc]], [bPB[pgr]], kc == 7)
            wrel(s_g)
            g1, bg1 = tmp()
            g2, bg2 = tmp()
            act(g1[:], PB[pga][:], AF.Sigmoid, [bPB[pga], bconst], [bg1], bias=bgs[:, l, n:n + 1], scale=1.0)
            act(g2[:], PB[pgr][:], AF.Sigmoid, [bPB[pgr], bconst], [bg2], bias=bgs[:, l, 8 + n:9 + n], scale=1.0)
            tt("dve", g1[:], g1[:], PB[pya][:], ALU.mult, [bg1, bPB[pya]], [bg1])
            tt("dve", g2[:], g2[:], PB[pyr][:], ALU.mult, [bg2, bPB[pyr]], [bg2])
            tt(PL, xn[:, n, :], g1[:], g2[:], ALU.add, [bg1, bg2], [bxn[n]])
        wrel(s_ba)
        wrel(s_br)
        s0 = wget(l, ("wo", 0))
        s1 = wget(l, ("wo", 1))
        w0 = ws[s0][:].rearrange("p (k n) -> p k n", k=4)
        w1 = ws[s1][:].rearrange("p (k n) -> p k n", k=4)
        for n in range(8):
            ns = slice(n * 128, (n + 1) * 128)
            pb = n % 6
            for kc in range(8):
                wsrc, sidx = (w0, s0) if kc < 4 else (w1, s1)
                mm(PB[pb][:], wsrc[:, kc % 4, ns], xn[:, kc, :], kc == 0, kc == 7, [bws[sidx], bxn[kc]], [bPB[pb]], kc == 7)
            tt("dve", xT[:, n, :], xT[:, n, :], PB[pb][:], ALU.add, [bx[n], bPB[pb]], [bx[n]])
        wrel(s0)
        wrel(s1)
        ffn(l, "f2", l * 3 + 2)
        P.dma("sp", Xd.ap()[t], xT[:], reads=bx, writes=[bXd[t]])

    def final_pass(t):
        P.dma("sp", xT[:], Xd.ap()[t], reads=[bXd[t]], writes=bx)
        pb = 6
        for c in range(8):
            sq = PT[c % NPT]
            act(sq[:], xT[:, c, :], AF.Square, [bx[c]], [bPT[c % NPT]])
            mm(PB[pb][:], onesM[:], sq[:], c == 0, c == 7, [bPT[c % NPT], bconst], [bPB[pb]], True)
        rs, brs = tmp()
        rsqrt_act(rs[:], PB[pb][:], [bPB[pb]], [brs])
        for c in range(8):
            stt("dve", xT[:, c, :], xT[:, c, :], nrm[:, DEPTH * 3, c:c + 1], rs[:],
                ALU.mult, ALU.mult, [bx[c], brs, bconst], [bx[c]])
        for sub in range(4):
            xi = xi_n[0] % 2
            xi_n[0] += 1
            for g in range(2):
                pbk = (sub * 2 + g) % 6
                for cc in range(4):
                    c = g * 4 + cc
                    P.op("pe", lambda e, pbk=pbk, cc=cc, c=c, sub=sub: e.transpose(
                        PB[pbk][:, cc * 128:(cc + 1) * 128], xT[:, c, sub * 128:(sub + 1) * 128], identf[:]),
                        reads=[bx[c], bconst], writes=[bPB[pbk]], inc=(cc == 3))
                cp("dve" if g == 0 else "act", XI[xi][:, g * 512:(g + 1) * 512], PB[pbk][:], [bPB[pbk]], [bXI[xi]])
            r0 = t * TT + sub * 128
            by = Buf()
            bYs.append(by)
            P.dma("sp", y_out.ap()[r0:r0 + 128, :], XI[xi][:], reads=[bXI[xi]], writes=[by])

    if RUN_B:
        layer_tables(0)
        phase_S_finish(0)
        for t in order_B:
            phase_B(0, t)
    if RUN_A:
        layer_tables(1)
        for t in order_A:
            phase_A(1, t)
        phase_S_prompt(1)
    if KIND == "last":
        for t in range(NT):
            final_pass(t)
    assert WL.nuse == len(seq), (WL.nuse, len(seq))
    P.final_wait("sp", bYs)

    with nc.Block() as block:
        @block.tensor
        def _(e):
            P.replay(e, "pe")

        @block.scalar
        def _(e):
            P.replay(e, "act")

        @block.vector
        def _(e):
            P.replay(e, "dve")

        @block.gpsimd
        def _(e):
            P.replay(e, "pool")

        @block.sync
        def _(e):
            P.replay(e, "sp")

    stack.close()
    return nc


def run_model(inp, NT_S, NT_P, DEPTH):
    T_S = NT_S * TT
    T_PL = NT_P * TT
    xs = np.asarray(inp["x_sample"], np.float32)
    xp = np.asarray(inp["x_prompt"], np.float32)
    assert xs.shape == (NCORES, T_S, D) and xp.shape == (1, T_PL * NCORES, D)
    pidx = np.arange(128) % 64
    pperm = (pidx + 32) % 64
    ident = np.eye(128, dtype=np.float32)

    def params(lb, la):
        norms = np.empty((128, 7, 8), np.float32)
        qkg = np.empty((128, 2, 4), np.float32)
        bgate = np.empty((128, 2, 16), np.float32)
        retg = np.empty((128, 2, 512), np.float32)
        dec = np.empty((128, 2, 2, 4), np.float32)
        for sl, l in ((0, lb), (1, la)):
            for k, nm in enumerate(("ffn1_norm", "mix_norm", "ffn2_norm")):
                norms[:, sl * 3 + k, :] = np.asarray(inp[nm][l], np.float32).reshape(8, 128).T
            qn = np.asarray(inp["q_norm"][l], np.float32)
            kn = np.asarray(inp["k_norm"][l], np.float32)
            qkg[:, sl, 0] = qn[pidx]
            qkg[:, sl, 1] = qn[pperm]
            qkg[:, sl, 2] = kn[pidx]
            qkg[:, sl, 3] = kn[pperm]
            bgate[:, sl, :] = np.asarray(inp["b_gate"][l], np.float32).reshape(16, 128).T
            retg[:, sl, :] = np.asarray(inp["ret_norm"][l], np.float32)[None, :]
            dec[:, sl, 0, :] = np.asarray(inp["ret_decay_fwd"][l], np.float32)[None, :]
            dec[:, sl, 1, :] = np.asarray(inp["ret_decay_bwd"][l], np.float32)[None, :]
        norms[:, 6, :] = np.asarray(inp["final_norm"], np.float32).reshape(8, 128).T
        return {"norms": norms, "qkg": qkg, "bgate": bgate, "retg": retg, "dec": dec, "ident": ident}

    cs_c, ct_c = [], []
    for c in range(NCORES):
        pos = np.concatenate([np.arange(T_S), c * T_PL + np.arange(T_PL)])
        cs = rope_tables(pos)
        cs_c.append(np.ascontiguousarray(cs.reshape(128, 2, NT_S + NT_P, TT).transpose(2, 0, 1, 3)))
        ct_c.append(const_tables(c, NT_P * 4))

    progs = {}

    def prog(kind):
        if kind not in progs:
            progs[kind] = build_program(NT_S, NT_P, kind)
        return progs[kind]

    STATE = ["Xe", "Hd", "Qd", "RQd", "RKd", "RVd", "KTs", "VEs", "KVd"]
    state = None
    blob_prev = None
    res = None
    for k in range(DEPTH + 1):
        kind = "first" if k == 0 else ("last" if k == DEPTH else "mid")
        lb = max(k - 1, 0)
        la = min(k, DEPTH - 1)
        pr = params(lb, la)
        blob_a = arrange_layer(inp, la) if kind != "last" else None
        in_maps = []
        if state is not None:
            ktg = np.concatenate([state[r]["KTl1"] for r in range(NCORES)], axis=0)
            veg = np.concatenate([state[r]["VEl1"] for r in range(NCORES)], axis=0)
            ago = np.concatenate([state[r]["AGi1"] for r in range(NCORES)], axis=0)
        for c in range(NCORES):
            m = dict(pr)
            m["cs"] = cs_c[c]
            m["ctab"] = ct_c[c]
            if kind == "first":
                m["x_in"] = np.ascontiguousarray(np.concatenate([xs[c], xp[0, c * T_PL:(c + 1) * T_PL]], axis=0))
            else:
                for nm in STATE:
                    m[nm + "0"] = state[c][nm + "1"]
                m["KTg0"] = ktg
                m["VEg0"] = veg
                m["AGo0"] = ago
                m["wf0"] = np.ascontiguousarray(blob_prev[:, FA:])
            if kind != "last":
                m["wf1"] = np.ascontiguousarray(blob_a[:, :FA])
            in_maps.append(m)
        res = run_bass_kernel_spmd(prog(kind), in_maps, core_ids=list(range(NCORES)))
        state = res.results
        blob_prev = blob_a
    ys = np.empty((NCORES, T_S, D), np.float32)
    yp = np.empty((1, T_PL * NCORES, D), np.float32)
    for c in range(NCORES):
        y = np.asarray(res.results[c]["y_out"], np.float32)
        ys[c] = y[:T_S]
        yp[0, c * T_PL:(c + 1) * T_PL] = y[T_S:]
    return yp, ys


def kernel(**inputs):
    return run_model(inputs, NT_S=8, NT_P=4, DEPTH=4)
```

```python
import math
import os
from contextlib import ExitStack

import numpy as np
import ml_dtypes

import concourse.bass as bass
import concourse.mybir as mybir
from concourse.bass_utils import run_bass_kernel_spmd

F32 = mybir.dt.float32
BF16 = mybir.dt.bfloat16
AF = mybir.ActivationFunctionType
ALU = mybir.AluOpType
AX = mybir.AxisListType

NCORES = 8
D = 1024
DFF = 2816
NJ = DFF // 128
HD = 64
C = 128
TT = 512
EPS = 1e-6
GRID_W = 64
ROPE_THETA = 10000.0

WSIZES = [("f1w13", 45056), ("f1w2", 22528), ("winfm", 26624), ("wrv", 4096), ("wav", 1024),
          ("wrg", 4096), ("wgate", 16384), ("wba", 4096), ("wbr", 4096), ("wo", 8192),
          ("f2w13", 45056), ("f2w2", 22528)]
WOFF = {}
_o = 0
for _n, _s in WSIZES:
    WOFF[_n] = _o
    _o += _s
FL = _o
A_NAMES = ["f1w13", "f1w2", "winfm", "wrv", "wav"]
FA = sum(sz for n, sz in WSIZES if n in A_NAMES)
FB = FL - FA
WPART = {n: (1 if n in A_NAMES else 0) for n, _ in WSIZES}
WREL = {n: (WOFF[n] if n in A_NAMES else WOFF[n] - FA) for n, _ in WSIZES}
SLOT = 4096
NSLOT = 5


class Sem:
    def __init__(self, h):
        self.h = h
        self.n = 0


class Buf:
    __slots__ = ("w", "r", "ro", "excl")

    def __init__(self, ro=False, excl=False):
        self.w = {}
        self.r = {}
        self.ro = ro
        self.excl = excl


class Eng:
    def __init__(self, name):
        self.name = name
        self.ops = []
        self.waited = {}
        self.sem = None
        self.pend_r = []
        self.pend_w = []


class Prog:
    def __init__(self, nc, stack):
        self.nc = nc
        self.stack = stack
        self.nsem = 0
        self.E = {n: Eng(n) for n in ["pe", "act", "dve", "pool", "sp"]}
        for n in ["pe", "act", "dve", "pool"]:
            self.E[n].sem = self.newsem()
        self.dsems = {"sp": [self.newsem() for _ in range(28)], "pool": [self.newsem() for _ in range(10)]}
        self.di = {"sp": 0, "pool": 0}
        self.agsems = []

    def newsem(self):
        h = self.stack.enter_context(self.nc.semaphore(f"s{self.nsem}"))
        self.nsem += 1
        return Sem(h)

    def _waits(self, e, reads, writes):
        need = {}
        for b in reads:
            for s, v in b.w.items():
                if need.get(s, 0) < v:
                    need[s] = v
            if b.excl:
                for s, v in b.r.items():
                    if s is not e.sem and need.get(s, 0) < v:
                        need[s] = v
        for b in writes:
            for dct in (b.w, b.r):
                for s, v in dct.items():
                    if need.get(s, 0) < v:
                        need[s] = v
        for s, v in need.items():
            if e.name == "pe" and s is e.sem:
                continue
            if e.waited.get(s, 0) < v:
                e.ops.append(("w", s.h, v))
                e.waited[s] = v

    def op(self, en, fn, reads=(), writes=(), inc=True):
        e = self.E[en]
        self._waits(e, reads, writes)
        if inc:
            if e.sem.n >= 12000:
                e.sem = self.newsem()
            e.sem.n += 1
            s, v = e.sem, e.sem.n
            e.ops.append(("i", fn, s.h))
            for b in list(reads) + e.pend_r:
                if not b.ro:
                    b.r[s] = v
            for b in list(writes) + e.pend_w:
                b.w = {s: v}
                b.r = {}
            e.pend_r = []
            e.pend_w = []
        else:
            e.ops.append(("i", fn, None))
            e.pend_r += [b for b in reads if not b.ro]
            e.pend_w += list(writes)

    def dma(self, q, out, in_, reads=(), writes=()):
        e = self.E[q]
        sems = self.dsems[q]
        s = sems[self.di[q] % len(sems)]
        self.di[q] += 1
        if s.n > 0 and e.waited.get(s, 0) < s.n:
            e.ops.append(("w", s.h, s.n))
            e.waited[s] = s.n
        self._waits(e, reads, writes)
        s.n += 16
        e.ops.append(("d", out, in_, s.h))
        for b in reads:
            if not b.ro:
                b.r[s] = s.n
        for b in writes:
            b.w = {s: s.n}
            b.r = {}

    def allgather(self, in_t, out_t, reads=(), writes=()):
        e = self.E["pool"]
        self._waits(e, reads, writes)
        s = self.newsem()
        s.n = 1
        self.agsems.append(s)
        e.ops.append(("c", in_t, out_t, s.h))
        for b in reads:
            if not b.ro:
                b.r[s] = 1
        for b in writes:
            b.w = {s: 1}
            b.r = {}

    def final_wait(self, en, bufs):
        e = self.E[en]
        self._waits(e, bufs, bufs)
        for q in ("sp", "pool"):
            for s in self.dsems[q]:
                if s.n > 0 and e.waited.get(s, 0) < s.n:
                    e.ops.append(("w", s.h, s.n))
                    e.waited[s] = s.n
        for s in self.agsems:
            e.ops.append(("w", s.h, 1))
        for n in ("pe", "act", "dve", "pool"):
            s = self.E[n].sem
            if s.n > 0:
                e.ops.append(("w", s.h, s.n))

    def replay(self, eng, en):
        for o in self.E[en].ops:
            k = o[0]
            if k == "w":
                eng.wait_ge(o[1], o[2])
            elif k == "i":
                ins = o[1](eng)
                if o[2] is not None:
                    ins.then_inc(o[2], 1)
            elif k == "d":
                eng.dma_start(out=o[1], in_=o[2]).then_inc(o[3], 16)
            elif k == "c":
                eng.collective_compute(
                    "AllGather", ALU.bypass, replica_groups=[list(range(NCORES))],
                    ins=[o[1].ap().opt()], outs=[o[2].ap().opt()],
                ).then_inc(o[3])


def _lhsT_chunks(W, cols_list):
    K = W.shape[0]
    kc = K // 128
    Wr = W.reshape(kc, 128, -1)
    out = np.empty((128, len(cols_list), kc, 128), np.float32)
    for i, cols in enumerate(cols_list):
        out[:, i] = Wr[:, :, cols].transpose(1, 0, 2)
    return out


def _perm64(cols):
    cols = np.asarray(cols)
    return np.concatenate([cols[32:64], cols[0:32]])


def arrange_layer(inp, l):
    ar = np.arange
    parts = {}
    for k, n13, n2 in (("f1", "ffn1_w13", "ffn1_w2"), ("f2", "ffn2_w13", "ffn2_w2")):
        W13 = inp[n13][l]
        cl = []
        for j in range(NJ):
            cl.append(ar(128 * j, 128 * j + 128))
            cl.append(DFF + ar(128 * j, 128 * j + 128))
        parts[k + "w13"] = _lhsT_chunks(W13, cl).reshape(128, -1)
        parts[k + "w2"] = _lhsT_chunks(inp[n2][l], [ar(128 * n, 128 * n + 128) for n in range(8)]).reshape(128, -1)
    Win = inp["w_in"][l]
    cl = []
    for c in range(4):
        h0 = ar(64 * c, 64 * c + 64)
        h1 = ar(64 * (4 + c), 64 * (4 + c) + 64)
        cl.append(np.concatenate([h0, h1]))
        cl.append(np.concatenate([_perm64(h0), _perm64(h1)]))
    k0 = 512 + ar(0, 64)
    k1 = 512 + ar(64, 128)
    cl.append(np.concatenate([k0, k1]))
    cl.append(np.concatenate([_perm64(k0), _perm64(k1)]))
    for base in (768, 1024):
        for h in range(4):
            hc = base + ar(64 * h, 64 * h + 64)
            cl.append(np.concatenate([hc, hc]))
            cl.append(np.concatenate([_perm64(hc), _perm64(hc)]))
    parts["winfm"] = _lhsT_chunks(Win, cl).reshape(128, -1)
    Wr = Win.reshape(8, 128, -1)
    parts["wrv"] = Wr[:, :, 1280:1792].transpose(1, 0, 2).reshape(128, -1)
    parts["wav"] = Wr[:, :, 640:768].transpose(1, 0, 2).reshape(128, -1)
    parts["wrg"] = Wr[:, :, 1792:2304].transpose(1, 0, 2).reshape(128, -1)
    cl = []
    for n in range(8):
        cl.append(ar(128 * n, 128 * n + 128))
        cl.append(1024 + ar(128 * n, 128 * n + 128))
    parts["wgate"] = _lhsT_chunks(inp["w_gate"][l], cl).reshape(128, -1)
    rows = []
    for c in range(4):
        rows.append(ar(64 * c, 64 * c + 64))
        rows.append(ar(64 * (4 + c), 64 * (4 + c) + 64))
    rows = np.concatenate(rows)
    parts["wba"] = inp["w_branch_attn"][l][rows].reshape(4, 128, 1024).transpose(1, 0, 2).reshape(128, -1)
    parts["wbr"] = inp["w_branch_ret"][l].reshape(4, 128, 1024).transpose(1, 0, 2).reshape(128, -1)
    parts["wo"] = inp["w_out"][l].reshape(8, 128, 1024).transpose(1, 0, 2).reshape(128, -1)
    blob = np.empty((128, FL), np.float32)
    for n, s in WSIZES:
        assert parts[n].shape == (128, s), (n, parts[n].shape)
        blob[:, WOFF[n]:WOFF[n] + s] = parts[n]
    return blob


def rope_tables(pos):
    pos = np.asarray(pos)
    row = (pos // GRID_W).astype(np.float32)
    col = (pos % GRID_W).astype(np.float32)
    freqs = (ROPE_THETA ** (-np.arange(16, dtype=np.float32) / 16)).astype(np.float32)
    ang = np.concatenate([row[:, None] * freqs, col[:, None] * freqs], axis=-1)
    cos = np.cos(ang).astype(np.float32).T
    sin = np.sin(ang).astype(np.float32).T
    c64 = np.concatenate([cos, cos], 0)
    s64 = np.concatenate([-sin, sin], 0)
    out = np.empty((128, 2, len(pos)), np.float32)
    out[:, 0] = np.concatenate([c64, c64], 0)
    out[:, 1] = np.concatenate([s64, s64], 0)
    return out


def const_tables(core, nch_p):
    i = np.arange(128, dtype=np.float32)
    jj = i[:, None]
    ii = i[None, :]
    t = np.zeros((128, 9, 128), np.float32)
    t[:, 0] = np.maximum(ii - jj, 0)
    t[:, 1] = (ii >= jj)
    t[:, 2] = np.maximum(jj - ii, 0)
    t[:, 3] = (jj > ii)
    t[0:64, 4] = (ii + 1.0)
    t[64:128, 4] = (C - ii)
    t[:, 5, 0:64] = (C - 1.0 - jj)
    t[:, 5, 64:128] = jj
    t[:, 6] = float(C)
    cp = np.arange(8, dtype=np.float32)
    ef = nch_p * C * (core - 1 - cp)
    mf = (cp < core)
    eb = nch_p * C * (cp - core - 1)
    mb = (cp > core)
    t[0:64, 7, 0:8] = np.where(mf, ef, 0.0)
    t[64:128, 7, 0:8] = np.where(mb, eb, 0.0)
    t[0:64, 8, 0:8] = mf
    t[64:128, 8, 0:8] = mb
    return t


def build_program(NT_S, NT_P, KIND):
    DEPTH = 2
    RUN_B = KIND in ("mid", "last")
    RUN_A = KIND in ("first", "mid")
    NT = NT_S + NT_P
    T_S = NT_S * TT
    T_PL = NT_P * TT
    T_P = T_PL * NCORES
    NCH = NT * 4
    NCH_P = NT_P * 4
    nc = bass.Bass("TRN2", target_bir_lowering=False)
    STAGE = 9
    PL = "dve"
    KSUB = 9
    stack = ExitStack()
    P = Prog(nc, stack)

    def dram(name, shape, dt, kind=None):
        if kind:
            return nc.dram_tensor(name, shape, dt, kind=kind)
        return nc.dram_tensor(name, shape, dt)

    KIN = "ExternalInput" if RUN_B else None
    KOUT = "ExternalOutput" if RUN_A else None

    def pair(name, shape, dt):
        return [dram(name + "0", shape, dt, KIN), dram(name + "1", shape, dt, KOUT)]

    x_in = dram("x_in", [NT * TT, D], F32, "ExternalInput") if KIND == "first" else None
    y_out = dram("y_out", [NT * TT, D], F32, "ExternalOutput") if KIND == "last" else None
    wf = [dram("wf0", [128, FB], F32, "ExternalInput") if RUN_B else None,
          dram("wf1", [128, FA], F32, "ExternalInput") if RUN_A else None]
    cs_in = dram("cs", [NT, 128, 2, TT], F32, "ExternalInput")
    ct_in = dram("ctab", [128, 9, 128], F32, "ExternalInput")
    nrm_in = dram("norms", [128, DEPTH * 3 + 1, 8], F32, "ExternalInput")
    qk_in = dram("qkg", [128, DEPTH, 4], F32, "ExternalInput")
    bg_in = dram("bgate", [128, DEPTH, 16], F32, "ExternalInput")
    rg_in = dram("retg", [128, DEPTH, 512], F32, "ExternalInput")
    dec_in = dram("dec", [128, DEPTH, 2, 4], F32, "ExternalInput")
    id_in = dram("ident", [128, 128], F32, "ExternalInput")
    wb = [dram("wb0", [128, FB], BF16), dram("wb1", [128, FA], BF16)]
    Xd = dram("Xd", [NT, 128, 8, TT], F32)
    Xe = pair("Xe", [NT, 128, 8, TT], F32)
    Hd = pair("Hd", [NT, 128, 8, TT], BF16)
    Qd = pair("Qd", [NT, 128, 4, TT], BF16)
    RQd = pair("RQd", [NT, 128, 4, TT], BF16)
    RKd = pair("RKd", [NT, 128, 4, TT], BF16)
    RVd = pair("RVd", [NT, 128, 4, 512], BF16)
    KTs = pair("KTs", [128, T_S], BF16)
    VEs = pair("VEs", [T_S, 256], BF16)
    KTl = [None, dram("KTl1", [128, T_PL], BF16, KOUT)]
    VEl = [None, dram("VEl1", [T_PL, 256], BF16, KOUT)]
    KTg = [dram("KTg0", [NCORES * 128, T_PL], BF16, KIN), None]
    VEg = [dram("VEg0", [T_P, 256], BF16, KIN), None]
    KVd = pair("KVd", [NCH, 128, 512], F32)
    STd = dram("STd", [NCH, 128, 512], BF16)
    AGi = [None, dram("AGi1", [128, 512], F32, KOUT)]
    AGo = [dram("AGo0", [NCORES * 128, 512], F32, KIN), None]

    bXd = [Buf() for _ in range(NT)]
    bXe = [[Buf() for _ in range(NT)] for _ in range(2)]
    bH = [[Buf() for _ in range(NT)] for _ in range(2)]
    bQ = [[Buf() for _ in range(NT)] for _ in range(2)]
    bRQ = [[Buf() for _ in range(NT)] for _ in range(2)]
    bRK = [[Buf() for _ in range(NT)] for _ in range(2)]
    bRV = [[Buf() for _ in range(NT)] for _ in range(2)]
    bKTs = [Buf() for _ in range(2)]
    bVEs = [Buf() for _ in range(2)]
    bKTl = [Buf() for _ in range(2)]
    bVEl = [Buf() for _ in range(2)]
    bKTg = [Buf() for _ in range(2)]
    bVEg = [Buf() for _ in range(2)]
    bKV = [[Buf() for _ in range(NCH)] for _ in range(2)]
    bST = [Buf() for _ in range(NCH)]
    bAGi = [Buf() for _ in range(2)]
    bAGo = [Buf() for _ in range(2)]
    bWb = [Buf(ro=True) for _ in range(DEPTH)]
    bYs = []

    def sb(name, shape, dt):
        return stack.enter_context(nc.sbuf_tensor(name, shape, dt))

    ws = [sb(f"ws{i}", [128, SLOT], BF16) for i in range(NSLOT)]
    bws = [Buf() for _ in range(NSLOT)]
    xT = sb("xT", [128, 8, TT], F32)
    bx = [Buf() for _ in range(8)]
    hT = sb("hT", [128, 8, TT], BF16)
    bh = [Buf() for _ in range(8)]
    xn = sb("xn", [128, 8, TT], BF16)
    bxn = [Buf() for _ in range(8)]
    gT = sb("gT", [128, NJ, TT], BF16)
    bg = [Buf() for _ in range(NJ)]
    QT = sb("QT", [128, 4, TT], BF16)
    bQT = [Buf() for _ in range(4)]
    RQ = sb("RQ", [128, 4, TT], BF16)
    bRQs = [Buf() for _ in range(4)]
    RK = sb("RK", [128, 4, TT], BF16)
    bRKs = [Buf() for _ in range(4)]
    RV = sb("RV", [128, 4, 512], BF16)
    bRVs = [Buf() for _ in range(4)]
    ST = sb("ST", [128, 4, 512], BF16)
    bSTs = [Buf() for _ in range(4)]
    CS = sb("CS", [128, 2, TT], F32)
    bCS = Buf()
    KO = sb("KO", [128, TT], BF16)
    bKO = Buf()
    VO = sb("VO", [128, 4, 256], BF16)
    bVO = Buf()
    NKV = 3
    KS = [sb(f"KS{i}", [128, TT], BF16) for i in range(NKV)]
    VS = [sb(f"VS{i}", [128, 4, 256], BF16) for i in range(NKV)]
    bKVs = [Buf() for _ in range(NKV)]
    bKVsV = [Buf() for _ in range(NKV)]
    NPT = 4
    PT = [sb(f"PT{i}", [128, TT], BF16) for i in range(NPT)]
    bPT = [Buf() for _ in range(NPT)]
    attnT = sb("attnT", [128, 4, TT], BF16)
    battn = [Buf() for _ in range(4)]
    yrT = sb("yrT", [128, 4, TT], BF16)
    byr = [Buf() for _ in range(4)]
    NTMP = 6
    TM = [sb(f"TM{i}", [128, TT], F32) for i in range(NTMP)]
    bTM = [Buf() for _ in range(NTMP)]
    tmi = [0]

    def tmp():
        i = tmi[0] % NTMP
        tmi[0] += 1
        return TM[i], bTM[i]

    RA = sb("RA", [128, TT], F32)
    bRA = Buf()
    XI = [sb(f"XI{i}", [128, D], F32) for i in range(2)]
    bXI = [Buf() for _ in range(2)]
    DT_ = sb("DTab", [128, 4, 128], F32)
    QD = sb("QDtab", [128, 4, 128], F32)
    KD = sb("KDtab", [128, 4, 128], F32)
    GD = sb("GDtab", [128, 4, 128], F32)
    CO = sb("COtab", [128, 4, 8], F32)
    bDec = Buf()
    RGt = sb("RGt", [128, 512], F32)
    bRGt = Buf()
    ctab = sb("ctab_s", [128, 9, 128], F32)
    nrm = sb("nrm_s", [128, DEPTH * 3 + 1, 8], F32)
    qkg = sb("qkg_s", [128, DEPTH, 4], F32)
    bgs = sb("bg_s", [128, DEPTH, 16], F32)
    decs = sb("dec_s", [128, DEPTH, 2, 4], F32)
    lgall = sb("lgall", [128, 2, 4], F32)
    lgcat = sb("lgcat", [128, 4], F32)
    identf = sb("identf", [128, 128], F32)
    identb = sb("identb", [128, 128], BF16)
    onesM = sb("onesM", [128, 128], BF16)
    blk1 = sb("blk1", [128, 128], BF16)
    epsb = sb("epsb", [128, 1], F32)
    oneb = sb("oneb", [128, 1], F32)
    sm = sb("smalls", [128, 64], F32)
    bsm = Buf()
    Sacc = sb("Sacc", [128, 512], F32)
    bSacc = Buf()
    Sin = sb("Sin", [128, 512], F32)
    bSin = Buf()
    bconst = Buf()

    PB = [stack.enter_context(nc.psum_tensor(f"pb{i}", [128, 512], F32)) for i in range(8)]
    bPB = [Buf(excl=True) for _ in range(8)]

    seq = []

    def seq_ffn(l, k):
        for i in range(11):
            seq.append((l, WREL[k + "w13"] + i * 4096, 4096, (k + "w13", i)))
        for n in range(8):
            seq.append((l, WREL[k + "w2"] + n * 2816, 2816, (k + "w2", n)))

    order_A = list(range(NT_S, NT)) + list(range(NT_S))
    order_B = list(range(NT))
    if RUN_B:
        for t in order_B:
            seq.append((0, WREL["wrg"], 4096, ("wrg", 0)))
            seq.append((0, WREL["wba"], 4096, ("wba", 0)))
            seq.append((0, WREL["wbr"], 4096, ("wbr", 0)))
            for n in range(8):
                seq.append((0, WREL["wgate"] + n * 2048, 2048, ("wgate", n)))
            seq.append((0, WREL["wo"], 4096, ("wo", 0)))
            seq.append((0, WREL["wo"] + 4096, 4096, ("wo", 1)))
            seq_ffn(0, "f2")
    if RUN_A:
        for t in order_A:
            seq_ffn(1, "f1")
            for i in range(13):
                seq.append((1, WREL["winfm"] + i * 2048, 2048, ("winfm", i)))
            seq.append((1, WREL["wrv"], 4096, ("wrv", 0)))
            seq.append((1, WREL["wav"], 1024, ("wav", 0)))

    class WL:
        free = list(range(NSLOT))
        nload = 0
        nuse = 0
        slot_of = {}

    def wl_topup():
        while WL.free and WL.nload < len(seq) and WL.nload < WL.nuse + NSLOT:
            l, off, size, key = seq[WL.nload]
            s = WL.free.pop(0)
            P.dma("sp", ws[s][:, 0:size], wb[l].ap()[:, off:off + size], reads=[bWb[l]], writes=[bws[s]])
            WL.slot_of[WL.nload] = s
            WL.nload += 1

    def wget(l, key):
        i = WL.nuse
        while not (seq[i][0] == l and seq[i][3] == key):
            assert STAGE < 9 or KSUB < 9, (seq[i], l, key)
            if i >= WL.nload:
                wl_topup()
            s_ = WL.slot_of.pop(i)
            WL.nuse += 1
            WL.free.append(s_)
            i = WL.nuse
        if i >= WL.nload:
            wl_topup()
        assert i < WL.nload, "weight loader deadlock"
        WL.nuse += 1
        s = WL.slot_of.pop(i)
        return s

    def wrel(s):
        WL.free.append(s)
        wl_topup()

    def mm(out, lhsT, rhs, start, stop, reads, writes, inc):
        P.op("pe", lambda e: e.matmul(out, lhsT=lhsT, rhs=rhs, start=start, stop=stop),
             reads=reads, writes=writes, inc=inc)

    def act(out, in_, func, reads, writes, bias=None, scale=None):
        kw = {}
        if bias is not None:
            kw["bias"] = bias
        if scale is not None:
            kw["scale"] = scale
        P.op("act", lambda e: e.activation(out=out, in_=in_, func=func, **kw), reads=reads, writes=writes)

    def tt(en, out, in0, in1, op, reads, writes):
        P.op(en, lambda e: e.tensor_tensor(out=out, in0=in0, in1=in1, op=op), reads=reads, writes=writes)

    def stt(en, out, in0, scalar, in1, op0, op1, reads, writes):
        P.op(en, lambda e: e.scalar_tensor_tensor(out=out, in0=in0, scalar=scalar, in1=in1, op0=op0, op1=op1),
             reads=reads, writes=writes)

    def ts(en, out, in0, s1, s2, op0, op1, reads, writes):
        P.op(en, lambda e: e.tensor_scalar(out=out, in0=in0, scalar1=s1, scalar2=s2, op0=op0, op1=op1),
             reads=reads, writes=writes)

    def cp(en, out, in_, reads, writes):
        if en == "act":
            P.op(en, lambda e: e.activation(out=out, in_=in_, func=AF.Copy), reads=reads, writes=writes)
        else:
            P.op(en, lambda e: e.tensor_copy(out=out, in_=in_), reads=reads, writes=writes)

    def tsm(en, out, in0, s1, reads, writes):
        P.op(en, lambda e: e.tensor_scalar_mul(out=out, in0=in0, scalar1=s1), reads=reads, writes=writes)

    def rsqrt_act(out, in_, reads, writes, nrows=128):
        act(out, in_, AF.Ln, reads, writes, bias=epsb[0:nrows, 0:1], scale=1.0)
        act(out, out, AF.Exp, writes, writes, scale=-0.5)

    for l in range(DEPTH):
        if wf[l] is None:
            continue
        for r in range(8):
            P.dma("pool", wb[l].ap()[16 * r:16 * r + 16, :], wf[l].ap()[16 * r:16 * r + 16, :], writes=[bWb[l]])
    P.dma("sp", ctab[:], ct_in.ap()[:, :, :], writes=[bconst])
    P.dma("sp", nrm[:], nrm_in.ap()[:, :, :], writes=[bconst])
    P.dma("sp", qkg[:], qk_in.ap()[:, :, :], writes=[bconst])
    P.dma("sp", bgs[:], bg_in.ap()[:, :, :], writes=[bconst])
    P.dma("sp", decs[:], dec_in.ap()[:, :, :, :], writes=[bconst])
    P.dma("sp", identf[:], id_in.ap()[:, :], writes=[bconst])
    P.op("dve", lambda e: e.tensor_copy(out=identb[:], in_=identf[:]), reads=[bconst], writes=[bconst])
    P.op("dve", lambda e: e.memset(onesM[:], 1.0 / D), writes=[bconst])
    P.op("dve", lambda e: e.memset(blk1[:], 0.0), writes=[bconst])
    P.op("dve", lambda e: e.memset(blk1[0:64, 0:64], 1.0 / HD), writes=[bconst])
    P.op("dve", lambda e: e.memset(blk1[64:128, 64:128], 1.0 / HD), writes=[bconst])
    P.op("dve", lambda e: e.memset(epsb[:], EPS), writes=[bconst])
    P.op("dve", lambda e: e.memset(oneb[:], 1.0), writes=[bconst])
    P.op("dve", lambda e: e.memset(VO[:], 1.0), writes=[bVO])

    xi_n = [0]
    for t in (range(NT) if KIND == "first" else []):
        for sub in range(4):
            xi = xi_n[0] % 2
            xi_n[0] += 1
            r0 = t * TT + sub * 128
            P.dma("sp", XI[xi][:], x_in.ap()[r0:r0 + 128, :], writes=[bXI[xi]])
            for g in range(2):
                pb = (sub * 2 + g) % 7
                for cc in range(4):
                    c = g * 4 + cc
                    P.op("pe", lambda e, pb=pb, cc=cc, c=c, xi=xi: e.transpose(
                        PB[pb][:, cc * 128:(cc + 1) * 128], XI[xi][:, c * 128:(c + 1) * 128], identf[:]),
                        reads=[bXI[xi], bconst], writes=[bPB[pb]], inc=(cc == 3))
                src = PB[pb][:].rearrange("p (c t) -> p c t", c=4)
                cp("dve" if g == 0 else "act", xT[:, g * 4:(g + 1) * 4, sub * 128:(sub + 1) * 128], src,
                   reads=[bPB[pb]], writes=bx[g * 4:(g + 1) * 4])
        P.dma("sp", Xd.ap()[t], xT[:], reads=bx, writes=[bXd[t]])

    def rmsnorm_fm(gidx, dst, bdst, dst_f32=False):
        pb = 6
        for c in range(8):
            sq = PT[c % NPT]
            act(sq[:], xT[:, c, :], AF.Square, [bx[c]], [bPT[c % NPT]])
            mm(PB[pb][:], onesM[:], sq[:], c == 0, c == 7, [bPT[c % NPT], bconst], [bPB[pb]], True)
        rs, brs = tmp()
        rsqrt_act(rs[:], PB[pb][:], [bPB[pb]], [brs])
        for c in range(8):
            stt("dve", dst[:, c, :], xT[:, c, :], nrm[:, gidx, c:c + 1], rs[:], ALU.mult, ALU.mult,
                [bx[c], brs, bconst], [bdst[c]])

    def ffn(l, k, gidx):
        rmsnorm_fm(gidx, xn, bxn)
        for i in range(11):
            s = wget(l, (k + "w13", i))
            w = ws[s][:].rearrange("p (j a k m) -> p j a k m", j=2, a=2, k=8)
            for jj in range(2):
                j = 2 * i + jj
                pa = (2 * (j % 3)) % 7
                pbk = pa + 1
                for a, pbx in ((0, pa), (1, pbk)):
                    for kc in range(8):
                        mm(PB[pbx][:], w[:, jj, a, kc, :], xn[:, kc, :], kc == 0, kc == 7,
                           [bws[s], bxn[kc]], [bPB[pbx]], kc == 7)
                sa, bsa = tmp()
                act(sa[:], PB[pa][:], AF.Silu, [bPB[pa]], [bsa])
                tt("dve", gT[:, j, :], sa[:], PB[pbk][:], ALU.mult, [bsa, bPB[pbk]], [bg[j]])
            wrel(s)
        for n in range(8):
            s = wget(l, (k + "w2", n))
            w = ws[s][:, 0:2816].rearrange("p (j m) -> p j m", j=NJ)
            pb = n % 6
            for j in range(NJ):
                mm(PB[pb][:], w[:, j, :], gT[:, j, :], j == 0, j == NJ - 1, [bws[s], bg[j]], [bPB[pb]], j == NJ - 1)
            wrel(s)
            stt("dve", xT[:, n, :], PB[pb][:], 0.5, xT[:, n, :], ALU.mult, ALU.add, [bPB[pb], bx[n]], [bx[n]])

    def layer_tables(l):
        rd = [bconst, bDec]
        wr = [bDec]
        act(lgall[:], decs[:, l, :, :], AF.Exp, rd, wr)
        act(lgall[:], lgall[:], AF.Ln, rd, wr, bias=oneb[:, 0:1], scale=-1.0)
        cp("dve", lgcat[0:64, :], lgall[0:64, 0, :], rd, wr)
        cp("dve", lgcat[64:128, :], lgall[64:128, 1, :], rd, wr)
        for h in range(4):
            t1, b1 = tmp()
            t2, b2 = tmp()
            act(t1[:, 0:128], ctab[:, 0, :], AF.Exp, rd, [b1], scale=lgall[:, 0, h:h + 1])
            tt("dve", t1[:, 0:128], t1[:, 0:128], ctab[:, 1, :], ALU.mult, [b1, bconst], [b1])
            act(t2[:, 0:128], ctab[:, 2, :], AF.Exp, rd, [b2], scale=lgall[:, 1, h:h + 1])
            tt("dve", t2[:, 0:128], t2[:, 0:128], ctab[:, 3, :], ALU.mult, [b2, bconst], [b2])
            tt("dve", DT_[:, h, :], t1[:, 0:128], t2[:, 0:128], ALU.add, [b1, b2] + rd, wr)
            act(QD[:, h, :], ctab[:, 4, :], AF.Exp, rd, wr, scale=lgcat[:, h:h + 1])
            act(KD[:, h, 0:64], ctab[:, 5, 0:64], AF.Exp, rd, wr, scale=lgall[:, 0, h:h + 1])
            act(KD[:, h, 64:128], ctab[:, 5, 64:128], AF.Exp, rd, wr, scale=lgall[:, 1, h:h + 1])
            act(GD[:, h, :], ctab[:, 6, :], AF.Exp, rd, wr, scale=lgcat[:, h:h + 1])
            act(CO[:, h, :], ctab[:, 7, 0:8], AF.Exp, rd, wr, scale=lgcat[:, h:h + 1])
            tt("dve", CO[:, h, :], CO[:, h, :], ctab[:, 8, 0:8], ALU.mult, rd, wr)
        P.dma("sp", RGt[:], rg_in.ap()[:, l, :], reads=[bRGt], writes=[bRGt])

    def phase_A(l, t):
        par = l % 2
        is_p = t >= NT_S
        P.dma("sp", xT[:], Xd.ap()[t], reads=[bXd[t]], writes=bx)
        P.dma("sp", CS[:], cs_in.ap()[t], writes=[bCS])
        ffn(l, "f1", l * 3 + 0)
        if KSUB < 2:
            P.dma("sp", Xe[1].ap()[t], xT[:], reads=bx, writes=[bXe[1][t]])
            return
        rmsnorm_fm(l * 3 + 1, hT, bh)
        P.dma("sp", Xe[1].ap()[t], xT[:], reads=bx, writes=[bXe[1][t]])
        P.dma("sp", Hd[1].ap()[t], hT[:], reads=bh, writes=[bH[1][t]])
        if KSUB < 3:
            return
        for i in range(13):
            s = wget(l, ("winfm", i))
            w = ws[s][:, 0:2048].rearrange("p (a k m) -> p a k m", a=2, k=8)
            pa = (2 * (i % 3))
            pp = pa + 1
            for a, pbx in ((0, pa), (1, pp)):
                for kc in range(8):
                    mm(PB[pbx][:], w[:, a, kc, :], hT[:, kc, :], kc == 0, kc == 7, [bws[s], bh[kc]], [bPB[pbx]], kc == 7)
            wrel(s)
            t1, b1 = tmp()
            t2, b2 = tmp()
            if i < 5:
                gcol = 0 if i < 4 else 2
                sq = PT[i % NPT]
                bsq = bPT[i % NPT]
                act(sq[:], PB[pa][:], AF.Square, [bPB[pa]], [bsq])
                mm(PB[6][:], blk1[:], sq[:], True, True, [bsq, bconst], [bPB[6]], True)
                rs, brs = tmp()
                rsqrt_act(rs[:], PB[6][:], [bPB[6]], [brs])
                stt("dve", t1[:], PB[pa][:], qkg[:, l, gcol:gcol + 1], CS[:, 0, :], ALU.mult, ALU.mult,
                    [bPB[pa], bCS, bconst], [b1])
                stt("dve", t2[:], PB[pp][:], qkg[:, l, gcol + 1:gcol + 2], CS[:, 1, :], ALU.mult, ALU.mult,
                    [bPB[pp], bCS, bconst], [b2])
                tt(PL, t1[:], t1[:], t2[:], ALU.add, [b1, b2], [b1])
                if i < 4:
                    tt(PL, QT[:, i, :], t1[:], rs[:], ALU.mult, [b1, brs], [bQT[i]])
                else:
                    tt(PL, KO[:], t1[:], rs[:], ALU.mult, [b1, brs], [bKO])
            else:
                h = (i - 5) % 4
                isk = i >= 9
                sc = 0.125 if isk else 1.0
                stt("dve", t1[:], PB[pa][:], sc, CS[:, 0, :], ALU.mult, ALU.mult, [bPB[pa], bCS], [b1])
                stt("dve", t2[:], PB[pp][:], sc, CS[:, 1, :], ALU.mult, ALU.mult, [bPB[pp], bCS], [b2])
                if not isk:
                    tt(PL, RQ[:, h, :], t1[:], t2[:], ALU.add, [b1, b2], [bRQs[h]])
                else:
                    tt(PL, t1[:], t1[:], t2[:], ALU.add, [b1, b2], [b1])
                    cp("act", RK[:, h, :], t1[:], [b1], [bRKs[h]])
                    for sub in range(4):
                        P.op("pe", lambda e, t1=t1, sub=sub: e.transpose(
                            PB[7][:, sub * 128:(sub + 1) * 128], t1[:, sub * 128:(sub + 1) * 128], identf[:]),
                            reads=[b1, bconst], writes=[bPB[7]], inc=(sub == 3))
                    src = PB[7][:].rearrange("p (s m) -> p s m", s=4)
                    for sub in range(4):
                        tt("dve", ST[:, sub, h * 128:(h + 1) * 128], src[:, sub, :], KD[:, h, :], ALU.mult,
                           [bPB[7], bDec], [bSTs[sub]])
        if KSUB < 4:
            return
        P.dma("sp", Qd[1].ap()[t], QT[:], reads=bQT, writes=[bQ[1][t]])
        P.dma("sp", RQd[1].ap()[t], RQ[:], reads=bRQs, writes=[bRQ[1][t]])
        P.dma("sp", RKd[1].ap()[t], RK[:], reads=bRKs, writes=[bRK[1][t]])
        if is_p:
            k0 = (t - NT_S) * TT
            P.dma("sp", KTl[par].ap()[:, k0:k0 + TT], KO[:], reads=[bKO], writes=[bKTl[par]])
        else:
            k0 = t * TT
            P.dma("sp", KTs[par].ap()[:, k0:k0 + TT], KO[:], reads=[bKO], writes=[bKTs[par]])
        s = wget(l, ("wrv", 0))
        w = ws[s][:].rearrange("p (k n) -> p k n", k=8)
        for sub in range(4):
            pb = sub % 6
            for kc in range(8):
                mm(PB[pb][:], hT[:, kc, sub * 128:(sub + 1) * 128], w[:, kc, :], kc == 0, kc == 7,
                   [bws[s], bh[kc]], [bPB[pb]], kc == 7)
            cp("act", RV[:, sub, :], PB[pb][:], [bPB[pb]], [bRVs[sub]])
        wrel(s)
        s = wget(l, ("wav", 0))
        w = ws[s][:, 0:1024].rearrange("p (k n) -> p k n", k=8)
        pb = 4
        for sub in range(4):
            for kc in range(8):
                mm(PB[pb][:, sub * 128:(sub + 1) * 128], hT[:, kc, sub * 128:(sub + 1) * 128], w[:, kc, :],
                   kc == 0, kc == 7, [bws[s], bh[kc]], [bPB[pb]], (kc == 7 and sub == 3))
        wrel(s)
        src = PB[pb][:].rearrange("p (s m) -> p s m", s=4)
        cp("dve", VO[:, :, 0:64], src[:, :, 0:64], [bPB[pb]], [bVO])
        cp("dve", VO[:, :, 192:256], src[:, :, 64:128], [bPB[pb]], [bVO])
        P.dma("sp", RVd[1].ap()[t], RV[:], reads=bRVs, writes=[bRV[1][t]])
        if is_p:
            k0 = (t - NT_S) * TT
            dst = VEl[par].ap()[k0:k0 + TT, :].rearrange("(s p) c -> p s c", p=128)
            P.dma("sp", dst, VO[:], reads=[bVO], writes=[bVEl[par]])
        else:
            k0 = t * TT
            dst = VEs[par].ap()[k0:k0 + TT, :].rearrange("(s p) c -> p s c", p=128)
            P.dma("sp", dst, VO[:], reads=[bVO], writes=[bVEs[par]])
        for sub in range(4):
            pb = 5 if sub % 2 == 0 else 6
            for h in range(4):
                mm(PB[pb][:, h * 128:(h + 1) * 128], ST[:, sub, h * 128:(h + 1) * 128], RV[:, sub, h * 128:(h + 1) * 128],
                   True, True, [bSTs[sub], bRVs[sub]], [bPB[pb]], h == 3)
            kv, bkv = tmp()
            cp("act", kv[:], PB[pb][:], [bPB[pb]], [bkv])
            ch = t * 4 + sub
            P.dma("sp", KVd[1].ap()[ch], kv[:], reads=[bkv], writes=[bKV[1][ch]])

    def scan_dir(chunks, fwd, init_from_sin, store, kvpar):
        r0, r1 = (0, 64) if fwd else (64, 128)
        order = chunks if fwd else chunks[::-1]
        if init_from_sin:
            cp("dve", Sacc[r0:r1, :], Sin[r0:r1, :], [bSin, bSacc], [bSacc])
        else:
            P.op("dve", lambda e: e.memset(Sacc[r0:r1, :], 0.0), reads=[bSacc], writes=[bSacc])
        for ch in order:
            if store:
                cp(PL, STo[r0:r1, :], Sacc[r0:r1, :], [bSacc, bSTo], [bSTo])
                P.dma("sp", STd.ap()[ch][r0:r1, :], STo[r0:r1, :], reads=[bSTo], writes=[bST[ch]])
            kv, bkv = tmp()
            P.dma("sp", kv[r0:r1, :], KVd[kvpar].ap()[ch][r0:r1, :], reads=[bKV[kvpar][ch]], writes=[bkv])
            tt("dve", Sacc[r0:r1, :], Sacc[r0:r1, :], GD[r0:r1, :, :].rearrange("p h m -> p (h m)"), ALU.mult,
               [bSacc, bDec], [bSacc])
            tt("dve", Sacc[r0:r1, :], Sacc[r0:r1, :], kv[r0:r1, :], ALU.add, [bSacc, bkv], [bSacc])

    STo = sb("STo", [128, 512], BF16)
    bSTo = Buf()

    def phase_S_prompt(l):
        par = l % 2
        pch = list(range(NT_S * 4, NCH))
        scan_dir(pch, True, False, False, 1)
        scan_dir(pch, False, False, False, 1)
        P.dma("sp", AGi[par].ap()[:, :], Sacc[:], reads=[bSacc], writes=[bAGi[par]])

    def phase_S_finish(l):
        par = l % 2
        pch = list(range(NT_S * 4, NCH))
        sch = list(range(NT_S * 4))
        P.op("dve", lambda e: e.memset(Sin[:], 0.0), reads=[bSin], writes=[bSin])
        for cpr in range(NCORES):
            a, ba = tmp()
            P.dma("sp", a[:], AGo[par].ap()[cpr * 128:(cpr + 1) * 128, :], reads=[bAGo[par]], writes=[ba])
            for h in range(4):
                stt("dve", Sin[:, h * 128:(h + 1) * 128], a[:, h * 128:(h + 1) * 128], CO[:, h, cpr:cpr + 1],
                    Sin[:, h * 128:(h + 1) * 128], ALU.mult, ALU.add, [ba, bSin, bDec], [bSin])
        scan_dir(pch, True, True, True, 0)
        scan_dir(pch, False, True, True, 0)
        scan_dir(sch, True, False, True, 0)
        scan_dir(sch, False, False, True, 0)

    kvn = [0]
    ptn = [0]

    def phase_B(l, t):
        par = l % 2
        is_p = t >= NT_S
        P.dma("sp", xT[:], Xe[0].ap()[t], reads=[bXe[0][t]], writes=bx)
        P.dma("sp", hT[:], Hd[0].ap()[t], reads=[bH[0][t]], writes=bh)
        P.dma("sp", QT[:], Qd[0].ap()[t], reads=[bQ[0][t]], writes=bQT)
        P.dma("sp", RQ[:], RQd[0].ap()[t], reads=[bRQ[0][t]], writes=bRQs)
        P.dma("sp", RK[:], RKd[0].ap()[t], reads=[bRK[0][t]], writes=bRKs)
        P.dma("sp", RV[:], RVd[0].ap()[t], reads=[bRV[0][t]], writes=bRVs)
        for sub in range(4):
            ch = t * 4 + sub
            P.dma("sp", ST[:, sub, :], STd.ap()[ch], reads=[bST[ch]], writes=[bSTs[sub]])
        s_rg = wget(l, ("wrg", 0))
        wrg = ws[s_rg][:].rearrange("p (k n) -> p k n", k=8)
        AT, bAT = KO, bKO
        QC, bQC = STo, bSTo
        r_sq, r_yn, b_sqyn = XI[0][:, 0:512], XI[0][:, 512:1024], bXI[0]
        r_sg, r_y2, b_sgy2 = XI[1][:, 0:512], XI[1][:, 512:1024], bXI[1]

        def ret_stages():
            pso, prg = 6, 7
            for sub in range(4):
                tsl = slice(sub * 128, (sub + 1) * 128)
                for kc in range(8):
                    mm(PB[prg][:], hT[:, kc, tsl], wrg[:, kc, :], kc == 0, kc == 7, [bws[s_rg], bh[kc]], [bPB[prg]], kc == 7)
                yield
                act(r_sg, PB[prg][:], AF.Silu, [bPB[prg]], [b_sgy2])
                yield
                for h in range(4):
                    mm(PB[pso][:, h * 128:(h + 1) * 128], RK[0:64, h, tsl], RQ[0:64, h, tsl], True, True,
                       [bRKs[h], bRQs[h]], [bPB[pso]], h == 3)
                tt("dve", QC[:].rearrange("p (h m) -> p h m", h=4), RQ[:, :, tsl], QD[:], ALU.mult,
                   list(bRQs) + [bDec], [bQC])
                yield
                tt("dve", AT[:], PB[pso][:], DT_[:].rearrange("p h m -> p (h m)"), ALU.mult, [bPB[pso], bDec], [bAT])
                yield
                for h in range(4):
                    hs = slice(h * 128, (h + 1) * 128)
                    mm(PB[pso][:, hs], AT[:, hs], RV[:, sub, hs], True, False, [bAT, bRVs[sub]], [bPB[pso]], False)
                    mm(PB[pso][:, hs], QC[:, hs], ST[:, sub, hs], False, True, [bQC, bSTs[sub]], [bPB[pso]], h == 3)
                yield
                o3 = PB[pso][:].rearrange("p (h m) -> p h m", h=4)
                P.op("dve", lambda e, o3=o3: e.tensor_reduce(out=sm[:, 0:4], in_=o3, axis=AX.X, op=ALU.add),
                     reads=[bPB[pso], bsm], writes=[bsm])
                act(r_sq, PB[pso][:], AF.Square, [bPB[pso]], [b_sqyn])
                yield
                P.op("dve", lambda e: e.tensor_reduce(out=sm[:, 4:8], in_=r_sq.rearrange("p (h m) -> p h m", h=4),
                                                      axis=AX.X, op=ALU.add), reads=[b_sqyn, bsm], writes=[bsm])
                tsm("dve", sm[:, 8:12], sm[:, 0:4], 1.0 / 128, [bsm], [bsm])
                tt("dve", sm[:, 12:16], sm[:, 8:12], sm[:, 8:12], ALU.mult, [bsm], [bsm])
                stt("dve", sm[:, 16:20], sm[:, 4:8], 1.0 / 128, sm[:, 12:16], ALU.mult, ALU.subtract, [bsm], [bsm])
                yield
                rsqrt_act(sm[:, 20:24], sm[:, 16:20], [bsm], [bsm])
                yield
                stt("dve", sm[:, 24:28], sm[:, 8:12], -1.0, sm[:, 20:24], ALU.mult, ALU.mult, [bsm], [bsm])
                for h in range(4):
                    hs = slice(h * 128, (h + 1) * 128)
                    ts("dve", r_yn[:, hs], PB[pso][:, hs], sm[:, 20 + h:21 + h], sm[:, 24 + h:25 + h], ALU.mult, ALU.add,
                       [bPB[pso], bsm], [b_sqyn])
                yield
                tt("dve", r_yn, r_yn, RGt[:], ALU.mult, [b_sqyn, bRGt], [b_sqyn])
                tt("dve", r_y2, r_yn, r_sg, ALU.mult, [b_sqyn, b_sgy2], [b_sgy2])
                yield
                for h in range(4):
                    P.op("pe", lambda e, h=h: e.transpose(
                        PB[prg][:, h * 128:(h + 1) * 128], r_y2[:, h * 128:(h + 1) * 128], identf[:]),
                        reads=[b_sgy2, bconst], writes=[bPB[prg]], inc=(h == 3))
                yield
                cp("act", yrT[:, :, tsl], PB[prg][:].rearrange("p (h m) -> p h m", h=4), [bPB[prg]], list(byr))
                yield

        ret_gen = ret_stages()
        unit_n = [0]
        nkb = (T_P if is_p else T_S) // TT
        for c in range(4):
            oa, ob = 4, 5
            units = [(kb, kt) for kb in range(nkb) for kt in range(4)]
            slots = {}

            def load_kb(kb):
                if kb in slots or kb >= nkb:
                    return
                ks = kvn[0] % NKV
                kvn[0] += 1
                if is_p:
                    r = kb // NT_P
                    k0 = (kb % NT_P) * TT
                    ksrc = KTg[par].ap()[r * 128:(r + 1) * 128, k0:k0 + TT]
                    vsrc = VEg[par].ap()[kb * TT:(kb + 1) * TT, :].rearrange("(s p) c -> p s c", p=128)
                    rdk, rdv = bKTg[par], bVEg[par]
                else:
                    ksrc = KTs[par].ap()[:, kb * TT:(kb + 1) * TT]
                    vsrc = VEs[par].ap()[kb * TT:(kb + 1) * TT, :].rearrange("(s p) c -> p s c", p=128)
                    rdk, rdv = bKTs[par], bVEs[par]
                P.dma("sp", KS[ks][:], ksrc, reads=[rdk], writes=[bKVs[ks]])
                P.dma("sp", VS[ks][:], vsrc, reads=[rdv], writes=[bKVsV[ks]])
                slots[kb] = ks

            def emit_qk(u):
                kb, kt = units[u]
                load_kb(kb)
                if kt == 0:
                    load_kb(kb + 1)
                ks = slots[kb]
                sa = 2 * (u % 2)
                sbk = sa + 1
                mm(PB[sa][:], KS[ks][0:64, kt * 128:(kt + 1) * 128], QT[0:64, c, :], True, True,
                   [bKVs[ks], bQT[c]], [bPB[sa]], True)
                mm(PB[sbk][:], KS[ks][64:128, kt * 128:(kt + 1) * 128], QT[64:128, c, :], True, True,
                   [bKVs[ks], bQT[c]], [bPB[sbk]], True)

            emit_qk(0)
            nu = len(units)
            for u in range(nu):
                kb, kt = units[u]
                ks = slots[kb]
                if u + 1 < nu:
                    emit_qk(u + 1)
                sa = 2 * (u % 2)
                sbk = sa + 1
                p0 = ptn[0] % NPT
                p1 = (ptn[0] + 1) % NPT
                ptn[0] += 2
                act(PT[p0][:], PB[sa][:], AF.Exp, [bPB[sa]], [bPT[p0]], scale=0.125)
                act(PT[p1][:], PB[sbk][:], AF.Exp, [bPB[sbk]], [bPT[p1]], scale=0.125)
                mm(PB[oa][:], VS[ks][:, kt, 0:128], PT[p0][:], u == 0, u == nu - 1, [bKVsV[ks], bPT[p0]], [bPB[oa]], True)
                mm(PB[ob][:], VS[ks][:, kt, 128:256], PT[p1][:], u == 0, u == nu - 1, [bKVsV[ks], bPT[p1]], [bPB[ob]], True)
                unit_n[0] += 1
                if unit_n[0] % 2 == 0:
                    next(ret_gen, None)
            ra, bra = RA, bRA
            P.op("dve", lambda e, ra=ra: e.reciprocal(out=ra[64:128, :], in_=PB[4][64:128, :]), reads=[bPB[4]], writes=[bra])
            P.op("dve", lambda e, ra=ra: e.reciprocal(out=ra[0:64, :], in_=PB[5][0:64, :]), reads=[bPB[5]], writes=[bra])
            tt("dve", attnT[0:64, c, :], PB[4][0:64, :], ra[64:128, :], ALU.mult, [bPB[4], bra], [battn[c]])
            tt("dve", attnT[64:128, c, :], PB[5][64:128, :], ra[0:64, :], ALU.mult, [bPB[5], bra], [battn[c]])
        for _ in ret_gen:
            pass
        wrel(s_rg)
        s_ba = wget(l, ("wba", 0))
        s_br = wget(l, ("wbr", 0))
        wba = ws[s_ba][:].rearrange("p (c n) -> p c n", c=4)
        wbr = ws[s_br][:].rearrange("p (c n) -> p c n", c=4)
        for n in range(8):
            ns = slice(n * 128, (n + 1) * 128)
            s_g = wget(l, ("wgate", n))
            wg = ws[s_g][:, 0:2048].rearrange("p (a k m) -> p a k m", a=2, k=8)
            base = 0 if n % 2 == 0 else 3
            pya, pyr, pga = base, base + 1, base + 2
            pgr = 6
            for c in range(4):
                mm(PB[pya][:], wba[:, c, ns], attnT[:, c, :], c == 0, c == 3, [bws[s_ba], battn[c]], [bPB[pya]], c == 3)
            for h in range(4):
                mm(PB[pyr][:], wbr[:, h, ns], yrT[:, h, :], h == 0, h == 3, [bws[s_br], byr[h]], [bPB[pyr]], h == 3)
            for kc in range(8):
                mm(PB[pga][:], wg[:, 0, kc, :], hT[:, kc, :], kc == 0, kc == 7, [bws[s_g], bh[kc]], [bPB[pga]], kc == 7)
            for kc in range(8):
                mm(PB[pgr][:], wg[:, 1, kc, :], hT[:, kc, :], kc == 0, kc == 7, [bws[s_g], bh[kc]], [bPB[pgr]], kc == 7)
            wrel(s_g)
            g1, bg1 = tmp()
            g2, bg2 = tmp()
            act(g1[:], PB[pga][:], AF.Sigmoid, [bPB[pga], bconst], [bg1], bias=bgs[:, l, n:n + 1], scale=1.0)
            act(g2[:], PB[pgr][:], AF.Sigmoid, [bPB[pgr], bconst], [bg2], bias=bgs[:, l, 8 + n:9 + n], scale=1.0)
            tt("dve", g1[:], g1[:], PB[pya][:], ALU.mult, [bg1, bPB[pya]], [bg1])
            tt("dve", g2[:], g2[:], PB[pyr][:], ALU.mult, [bg2, bPB[pyr]], [bg2])
            tt(PL, xn[:, n, :], g1[:], g2[:], ALU.add, [bg1, bg2], [bxn[n]])
        wrel(s_ba)
        wrel(s_br)
        s0 = wget(l, ("wo", 0))
        s1 = wget(l, ("wo", 1))
        w0 = ws[s0][:].rearrange("p (k n) -> p k n", k=4)
        w1 = ws[s1][:].rearrange("p (k n) -> p k n", k=4)
        for n in range(8):
            ns = slice(n * 128, (n + 1) * 128)
            pb = n % 6
            for kc in range(8):
                wsrc, sidx = (w0, s0) if kc < 4 else (w1, s1)
                mm(PB[pb][:], wsrc[:, kc % 4, ns], xn[:, kc, :], kc == 0, kc == 7, [bws[sidx], bxn[kc]], [bPB[pb]], kc == 7)
            tt("dve", xT[:, n, :], xT[:, n, :], PB[pb][:], ALU.add, [bx[n], bPB[pb]], [bx[n]])
        wrel(s0)
        wrel(s1)
        ffn(l, "f2", l * 3 + 2)
        P.dma("sp", Xd.ap()[t], xT[:], reads=bx, writes=[bXd[t]])

    def final_pass(t):
        P.dma("sp", xT[:], Xd.ap()[t], reads=[bXd[t]], writes=bx)
        pb = 6
        for c in range(8):
            sq = PT[c % NPT]
            act(sq[:], xT[:, c, :], AF.Square, [bx[c]], [bPT[c % NPT]])
            mm(PB[pb][:], onesM[:], sq[:], c == 0, c == 7, [bPT[c % NPT], bconst], [bPB[pb]], True)
        rs, brs = tmp()
        rsqrt_act(rs[:], PB[pb][:], [bPB[pb]], [brs])
        for c in range(8):
            stt("dve", xT[:, c, :], xT[:, c, :], nrm[:, DEPTH * 3, c:c + 1], rs[:],
                ALU.mult, ALU.mult, [bx[c], brs, bconst], [bx[c]])
        for sub in range(4):
            xi = xi_n[0] % 2
            xi_n[0] += 1
            for g in range(2):
                pbk = (sub * 2 + g) % 6
                for cc in range(4):
                    c = g * 4 + cc
                    P.op("pe", lambda e, pbk=pbk, cc=cc, c=c, sub=sub: e.transpose(
                        PB[pbk][:, cc * 128:(cc + 1) * 128], xT[:, c, sub * 128:(sub + 1) * 128], identf[:]),
                        reads=[bx[c], bconst], writes=[bPB[pbk]], inc=(cc == 3))
                cp("dve" if g == 0 else "act", XI[xi][:, g * 512:(g + 1) * 512], PB[pbk][:], [bPB[pbk]], [bXI[xi]])
            r0 = t * TT + sub * 128
            by = Buf()
            bYs.append(by)
            P.dma("sp", y_out.ap()[r0:r0 + 128, :], XI[xi][:], reads=[bXI[xi]], writes=[by])

    if RUN_B:
        layer_tables(0)
        phase_S_finish(0)
        for t in order_B:
            phase_B(0, t)
    if RUN_A:
        layer_tables(1)
        for t in order_A:
            phase_A(1, t)
        phase_S_prompt(1)
    if KIND == "last":
        for t in range(NT):
            final_pass(t)
    assert WL.nuse == len(seq), (WL.nuse, len(seq))
    P.final_wait("sp", bYs)

    with nc.Block() as block:
        @block.tensor
        def _(e):
            P.replay(e, "pe")

        @block.scalar
        def _(e):
            P.replay(e, "act")

        @block.vector
        def _(e):
            P.replay(e, "dve")

        @block.gpsimd
        def _(e):
            P.replay(e, "pool")

        @block.sync
        def _(e):
            P.replay(e, "sp")

    stack.close()
    return nc


def run_model(inp, NT_S, NT_P, DEPTH):
    T_S = NT_S * TT
    T_PL = NT_P * TT
    xs = np.asarray(inp["x_sample"], np.float32)
    xp = np.asarray(inp["x_prompt"], np.float32)
    assert xs.shape == (NCORES, T_S, D) and xp.shape == (1, T_PL * NCORES, D)
    pidx = np.arange(128) % 64
    pperm = (pidx + 32) % 64
    ident = np.eye(128, dtype=np.float32)

    def params(lb, la):
        norms = np.empty((128, 7, 8), np.float32)
        qkg = np.empty((128, 2, 4), np.float32)
        bgate = np.empty((128, 2, 16), np.float32)
        retg = np.empty((128, 2, 512), np.float32)
        dec = np.empty((128, 2, 2, 4), np.float32)
        for sl, l in ((0, lb), (1, la)):
            for k, nm in enumerate(("ffn1_norm", "mix_norm", "ffn2_norm")):
                norms[:, sl * 3 + k, :] = np.asarray(inp[nm][l], np.float32).reshape(8, 128).T
            qn = np.asarray(inp["q_norm"][l], np.float32)
            kn = np.asarray(inp["k_norm"][l], np.float32)
            qkg[:, sl, 0] = qn[pidx]
            qkg[:, sl, 1] = qn[pperm]
            qkg[:, sl, 2] = kn[pidx]
            qkg[:, sl, 3] = kn[pperm]
            bgate[:, sl, :] = np.asarray(inp["b_gate"][l], np.float32).reshape(16, 128).T
            retg[:, sl, :] = np.asarray(inp["ret_norm"][l], np.float32)[None, :]
            dec[:, sl, 0, :] = np.asarray(inp["ret_decay_fwd"][l], np.float32)[None, :]
            dec[:, sl, 1, :] = np.asarray(inp["ret_decay_bwd"][l], np.float32)[None, :]
        norms[:, 6, :] = np.asarray(inp["final_norm"], np.float32).reshape(8, 128).T
        return {"norms": norms, "qkg": qkg, "bgate": bgate, "retg": retg, "dec": dec, "ident": ident}

    cs_c, ct_c = [], []
    for c in range(NCORES):
        pos = np.concatenate([np.arange(T_S), c * T_PL + np.arange(T_PL)])
        cs = rope_tables(pos)
        cs_c.append(np.ascontiguousarray(cs.reshape(128, 2, NT_S + NT_P, TT).transpose(2, 0, 1, 3)))
        ct_c.append(const_tables(c, NT_P * 4))

    progs = {}

    def prog(kind):
        if kind not in progs:
            progs[kind] = build_program(NT_S, NT_P, kind)
        return progs[kind]

    STATE = ["Xe", "Hd", "Qd", "RQd", "RKd", "RVd", "KTs", "VEs", "KVd"]
    state = None
    blob_prev = None
    res = None
    for k in range(DEPTH + 1):
        kind = "first" if k == 0 else ("last" if k == DEPTH else "mid")
        lb = max(k - 1, 0)
        la = min(k, DEPTH - 1)
        pr = params(lb, la)
        blob_a = arrange_layer(inp, la) if kind != "last" else None
        in_maps = []
        if state is not None:
            ktg = np.concatenate([state[r]["KTl1"] for r in range(NCORES)], axis=0)
            veg = np.concatenate([state[r]["VEl1"] for r in range(NCORES)], axis=0)
            ago = np.concatenate([state[r]["AGi1"] for r in range(NCORES)], axis=0)
        for c in range(NCORES):
            m = dict(pr)
            m["cs"] = cs_c[c]
            m["ctab"] = ct_c[c]
            if kind == "first":
                m["x_in"] = np.ascontiguousarray(np.concatenate([xs[c], xp[0, c * T_PL:(c + 1) * T_PL]], axis=0))
            else:
                for nm in STATE:
                    m[nm + "0"] = state[c][nm + "1"]
                m["KTg0"] = ktg
                m["VEg0"] = veg
                m["AGo0"] = ago
                m["wf0"] = np.ascontiguousarray(blob_prev[:, FA:])
            if kind != "last":
                m["wf1"] = np.ascontiguousarray(blob_a[:, :FA])
            in_maps.append(m)
        res = run_bass_kernel_spmd(prog(kind), in_maps, core_ids=list(range(NCORES)))
        state = res.results
        blob_prev = blob_a
    ys = np.empty((NCORES, T_S, D), np.float32)
    yp = np.empty((1, T_PL * NCORES, D), np.float32)
    for c in range(NCORES):
        y = np.asarray(res.results[c]["y_out"], np.float32)
        ys[c] = y[:T_S]
        yp[0, c * T_PL:(c + 1) * T_PL] = y[T_S:]
    return yp, ys


def kernel(**inputs):
    return run_model(inputs, NT_S=8, NT_P=4, DEPTH=4)
```
